# Optimizing a Trainium2 kernel written in Bass

```python
import jax, jax.numpy as jnp
from jax import lax
import numpy as np


D_MODEL = 1024
BATCH = 4
SEQ = 4096
DEPTH = 1
DEC_BATCH = 8
DEC_SEQ = 64
PAST_LEN = 2048

CHUNK = 64
Q_BLOCK = 128
SB_HEAD_DIM = 64
SB_WIDTH = D_MODEL // 2
SB_HEADS = SB_WIDTH // SB_HEAD_DIM
SGU_GROUP_DIM = 64
SGU_WIDTH = D_MODEL - SB_WIDTH
SGU_GROUPS = SGU_WIDTH // SGU_GROUP_DIM
SGU_CHUNK = 128
D_MIX = SB_WIDTH + SGU_WIDTH
D_IN = 3 * SB_WIDTH + 2 * SGU_WIDTH
D_FF = 2816
FFN_RES = 0.5
EPS = 1e-6

kernel_name = 'stickbreak_sgu_macaron_stream_step'


def rms_norm(x, g):
    xf = x.astype(jnp.float32)
    y = xf * lax.rsqrt(jnp.mean(xf * xf, axis=-1, keepdims=True) + EPS)
    return (y * g.astype(jnp.float32)).astype(x.dtype)


def layer_norm(x, g, b):
    xf = x.astype(jnp.float32)
    mu = jnp.mean(xf, axis=-1, keepdims=True)
    xc = xf - mu
    y = xc * lax.rsqrt(jnp.mean(xc * xc, axis=-1, keepdims=True) + EPS)
    return (y * g.astype(jnp.float32) + b.astype(jnp.float32)).astype(x.dtype)


def macaron_ffn(x, pre_g, post_g, w_gate, w_up, w_down):
    h = rms_norm(x, pre_g)
    f = (jax.nn.silu(h @ w_gate) * (h @ w_up)) @ w_down
    return x + FFN_RES * rms_norm(f, post_g)


def stick_breaking_block(q, k, v, q_pos, k_pos):
    z = jnp.einsum('bqhd,bkhd->bhqk', q.astype(jnp.float32), k.astype(jnp.float32)) * (SB_HEAD_DIM ** -0.5)
    mask = k_pos[None, :] < q_pos[:, None]
    log_1m = jnp.where(mask, jax.nn.log_sigmoid(-z), 0.0)
    suffix = lax.cumsum(log_1m, axis=3, reverse=True)
    between = jnp.concatenate([suffix[..., 1:], jnp.zeros_like(suffix[..., :1])], axis=-1)
    a = jnp.where(mask, jnp.exp(jax.nn.log_sigmoid(z) + between), 0.0)
    return jnp.einsum('bhqk,bkhd->bqhd', a, v.astype(jnp.float32)).astype(v.dtype)


def stick_breaking_prompt(q, k, v):
    B, S, H, d = q.shape
    n_blk = S // Q_BLOCK
    k_pos = jnp.arange(S)
    q_blocks = q.reshape(B, n_blk, Q_BLOCK, H, d).swapaxes(0, 1)

    def one_block(args):
        q_blk, i = args
        q_pos = i * Q_BLOCK + jnp.arange(Q_BLOCK)
        return stick_breaking_block(q_blk, k, v, q_pos, k_pos)

    o = lax.map(one_block, (q_blocks, jnp.arange(n_blk)))
    return o.swapaxes(0, 1).reshape(B, S, H, d)


def stick_breaking_sample(q, k_new, v_new, past_k, past_v):
    P = past_k.shape[1]
    T = q.shape[1]
    k_all = jnp.concatenate([past_k.astype(k_new.dtype), k_new], axis=1)
    v_all = jnp.concatenate([past_v.astype(v_new.dtype), v_new], axis=1)
    q_pos = P + jnp.arange(T)
    k_pos = jnp.arange(P + T)
    return stick_breaking_block(q, k_all, v_all, q_pos, k_pos)


def sgu_mask(n):
    i = jnp.arange(n)
    return (i[None, :] // CHUNK) <= (i[:, None] // CHUNK)


def sgu_prompt(u, g, w_s, b_s):
    B, S = u.shape[:2]
    n = S // SGU_CHUNK
    gc = g.reshape(B, n, SGU_CHUNK, SGU_GROUPS, SGU_GROUP_DIM)
    w = w_s * sgu_mask(SGU_CHUNK)[None].astype(w_s.dtype)
    mixed = jnp.einsum('gij,bcjgd->bcigd', w, gc) + b_s.T[None, None, :, :, None]
    return u * mixed.reshape(B, S, SGU_WIDTH)


def sgu_sample(u, g, w_s, b_s):
    B, T = u.shape[:2]
    w = (w_s * sgu_mask(SGU_CHUNK)[None].astype(w_s.dtype))[:, :T, :T]
    mixed = jnp.einsum('gij,bjgd->bigd', w, g) + b_s[:, :T].T[None, :, :, None]
    return u * mixed.reshape(B, T, SGU_WIDTH)


def token_mix(x, p, past_k, past_v):
    B, T, _ = x.shape
    h = rms_norm(x, p['mix_pre_g'])
    z = h @ p['w_in']
    q, k, v, u, g = jnp.split(z, [SB_WIDTH, 2 * SB_WIDTH, 3 * SB_WIDTH, 3 * SB_WIDTH + SGU_WIDTH], axis=-1)
    q = q.reshape(B, T, SB_HEADS, SB_HEAD_DIM)
    k = k.reshape(B, T, SB_HEADS, SB_HEAD_DIM)
    v = v.reshape(B, T, SB_HEADS, SB_HEAD_DIM)
    u = jax.nn.gelu(u, approximate=False)
    g = layer_norm(jax.nn.gelu(g, approximate=False), p['sgu_ln_g'], p['sgu_ln_b'])
    g = g.reshape(B, T, SGU_GROUPS, SGU_GROUP_DIM)
    if past_k is None:
        o_sb = stick_breaking_prompt(q, k, v)
        o_sgu = sgu_prompt(u, g, p['sgu_w_s'], p['sgu_b_s'])
    else:
        o_sb = stick_breaking_sample(q, k, v, past_k, past_v)
        o_sgu = sgu_sample(u, g, p['sgu_w_s'], p['sgu_b_s'])
    merged = jnp.concatenate([rms_norm(o_sb.reshape(B, T, SB_WIDTH), p['g_out_sb']),
                              rms_norm(o_sgu, p['g_out_sgu'])], axis=-1)
    x = x + rms_norm(merged @ p['w_out'], p['mix_post_g'])
    return x, k, v, g


def encoder_layer(x, p, past_k, past_v):
    x = macaron_ffn(x, p['ffn1_pre_g'], p['ffn1_post_g'], p['ffn1_w_gate'], p['ffn1_w_up'], p['ffn1_w_down'])
    x, k, v, g = token_mix(x, p, past_k, past_v)
    x = macaron_ffn(x, p['ffn2_pre_g'], p['ffn2_post_g'], p['ffn2_w_gate'], p['ffn2_w_up'], p['ffn2_w_down'])
    return x, k, v, g


def setup_inputs(seed: int = 0) -> dict:
    key = jax.random.key(seed)
    ks = jax.random.split(key, 24)
    L = DEPTH

    def nrm(k, shape, scale):
        return scale * jax.random.normal(k, shape, jnp.float32)

    def gain(k, n):
        return 1.0 + 0.02 * jax.random.normal(k, (L, n), jnp.float32)

    return {
        'x_prompt': nrm(ks[0], (BATCH, SEQ, D_MODEL), 1.0),
        'x_sample': nrm(ks[1], (DEC_BATCH, DEC_SEQ, D_MODEL), 1.0),
        'cache_k_sb': nrm(ks[2], (L, DEC_BATCH, PAST_LEN, SB_HEADS, SB_HEAD_DIM), 1.0),
        'cache_v_sb': nrm(ks[3], (L, DEC_BATCH, PAST_LEN, SB_HEADS, SB_HEAD_DIM), 1.0),
        'ffn1_pre_g': gain(ks[4], D_MODEL),
        'ffn1_post_g': gain(ks[5], D_MODEL),
        'ffn1_w_gate': nrm(ks[6], (L, D_MODEL, D_FF), D_MODEL ** -0.5),
        'ffn1_w_up': nrm(ks[7], (L, D_MODEL, D_FF), D_MODEL ** -0.5),
        'ffn1_w_down': nrm(ks[8], (L, D_FF, D_MODEL), D_FF ** -0.5),
        'mix_pre_g': gain(ks[9], D_MODEL),
        'mix_post_g': gain(ks[10], D_MODEL),
        'w_in': nrm(ks[11], (L, D_MODEL, D_IN), D_MODEL ** -0.5),
        'sgu_ln_g': gain(ks[12], SGU_WIDTH),
        'sgu_ln_b': nrm(ks[13], (L, SGU_WIDTH), 0.02),
        'sgu_w_s': nrm(ks[14], (L, SGU_GROUPS, SGU_CHUNK, SGU_CHUNK), SGU_CHUNK ** -0.5),
        'sgu_b_s': 1.0 + nrm(ks[15], (L, SGU_GROUPS, SGU_CHUNK), 0.02),
        'g_out_sb': gain(ks[16], SB_WIDTH),
        'g_out_sgu': gain(ks[17], SGU_WIDTH),
        'w_out': nrm(ks[18], (L, D_MIX, D_MODEL), D_MIX ** -0.5),
        'ffn2_pre_g': gain(ks[19], D_MODEL),
        'ffn2_post_g': gain(ks[20], D_MODEL),
        'ffn2_w_gate': nrm(ks[21], (L, D_MODEL, D_FF), D_MODEL ** -0.5),
        'ffn2_w_up': nrm(ks[22], (L, D_MODEL, D_FF), D_MODEL ** -0.5),
        'ffn2_w_down': nrm(ks[23], (L, D_FF, D_MODEL), D_FF ** -0.5),
    }


def reference(x_prompt, x_sample, cache_k_sb, cache_v_sb,
              ffn1_pre_g, ffn1_post_g, ffn1_w_gate, ffn1_w_up, ffn1_w_down,
              mix_pre_g, mix_post_g, w_in, sgu_ln_g, sgu_ln_b, sgu_w_s, sgu_b_s,
              g_out_sb, g_out_sgu, w_out,
              ffn2_pre_g, ffn2_post_g, ffn2_w_gate, ffn2_w_up, ffn2_w_down):
    params = {
        'ffn1_pre_g': ffn1_pre_g, 'ffn1_post_g': ffn1_post_g,
        'ffn1_w_gate': ffn1_w_gate, 'ffn1_w_up': ffn1_w_up, 'ffn1_w_down': ffn1_w_down,
        'mix_pre_g': mix_pre_g, 'mix_post_g': mix_post_g, 'w_in': w_in,
        'sgu_ln_g': sgu_ln_g, 'sgu_ln_b': sgu_ln_b, 'sgu_w_s': sgu_w_s, 'sgu_b_s': sgu_b_s,
        'g_out_sb': g_out_sb, 'g_out_sgu': g_out_sgu, 'w_out': w_out,
        'ffn2_pre_g': ffn2_pre_g, 'ffn2_post_g': ffn2_post_g,
        'ffn2_w_gate': ffn2_w_gate, 'ffn2_w_up': ffn2_w_up, 'ffn2_w_down': ffn2_w_down,
    }
    xp = x_prompt
    xs = x_sample
    k_p, v_p, k_s, v_s, g_s = [], [], [], [], []
    for l in range(DEPTH):
        p = {name: arr[l] for name, arr in params.items()}
        xp, kp, vp, _ = encoder_layer(xp, p, None, None)
        xs, kq, vq, gq = encoder_layer(xs, p, cache_k_sb[l], cache_v_sb[l])
        k_p.append(kp)
        v_p.append(vp)
        k_s.append(kq)
        v_s.append(vq)
        g_s.append(gq)
    return (xp, xs, jnp.stack(k_p), jnp.stack(v_p), jnp.stack(k_s), jnp.stack(v_s), jnp.stack(g_s))
```

```python
import numpy as np
import ml_dtypes
from contextlib import ExitStack
import concourse.bass as bass
import concourse.mybir as mybir
from concourse.bass_utils import run_bass_kernel_spmd

F32 = mybir.dt.float32
BF16 = mybir.dt.bfloat16
AF = mybir.ActivationFunctionType
ALU = mybir.AluOpType
AX = mybir.AxisListType

D = 1024
DC = 8
FF = 2816
FC = 22
NP = 2048
NS = 64
NT = NP + NS
PAST = 2048
EPS = 1e-6
NEG = -30000.0
SGS = [(0, 512), (512, 512), (1024, 512), (1536, 512), (2048, 64)]
GROUPS = [(0, SGS[0:2]), (1024, SGS[2:5])]
GW = 1088
NCOL = 64

STOP_AFTER = None
MIX_STOP = 99
SKIP_FFN1 = False
NO_CC = False
OVERLAP_CC = True


class Buf:
    __slots__ = ("w", "r")

    def __init__(self):
        self.w = None
        self.r = {}


class Prog:
    ENG = ("pe", "act", "dve", "pool", "sp")

    def __init__(self):
        self.st = {e: [] for e in self.ENG}
        self.cnt = {}
        self.base = {}

    def _deps(self, reads, writes, extra):
        d = dict(self.base)

        def add(k, v):
            if d.get(k, 0) < v:
                d[k] = v
        for b in reads:
            if b.w is not None:
                add(*b.w)
        for b in writes:
            if b.w is not None:
                add(*b.w)
            for k, v in b.r.items():
                add(k, v)
        for t in extra:
            if t is not None:
                add(*t)
        return d

    def _post(self, tok, reads, writes):
        k, v = tok
        for b in reads:
            if b.r.get(k, 0) < v:
                b.r[k] = v
        for b in writes:
            b.w = tok
            b.r = {}

    def op(self, eng, fn, reads=(), writes=(), extra=(), sem=None, inc=1):
        d = self._deps(reads, writes, extra)
        sem = sem or eng
        self.cnt[sem] = self.cnt.get(sem, 0) + inc
        tok = (sem, self.cnt[sem])
        self.st[eng].append((fn, d, sem, inc))
        self._post(tok, reads, writes)
        return tok

    def dma(self, fn, sem, reads=(), writes=(), extra=(), eng="sp"):
        return self.op(eng, fn, reads, writes, extra, sem="d_" + sem, inc=16)

    def barrier(self):
        self.base = dict(self.cnt)

    def emit(self, nc, block, sems):
        def run(name):
            def body(e):
                seen = {}
                for fn, d, sem, inc in self.st[name]:
                    for k, v in d.items():
                        if seen.get(k, 0) >= v:
                            continue
                        e.wait_ge(sems[k], v)
                        seen[k] = v
                    fn(e).then_inc(sems[sem], inc)
                if name == "sp":
                    for k, v in self.cnt.items():
                        if seen.get(k, 0) < v:
                            e.wait_ge(sems[k], v)
            return body
        block.tensor(run("pe"))
        block.scalar(run("act"))
        block.vector(run("dve"))
        block.gpsimd(run("pool"))
        block.sync(run("sp"))


class Arena:
    def __init__(self, ap32):
        self.ap = ap32
        self.off = 0
        self.cap = ap32.shape[1]

    def reset(self, off=0):
        self.off = off

    def f32(self, *shape):
        fr = shape[1:]
        n = int(np.prod(fr))
        v = self.ap[:, self.off:self.off + n]
        self.off += n
        assert self.off <= self.cap, (self.off, self.cap)
        return self._shape(v, fr)

    def bf16(self, *shape):
        fr = shape[1:]
        n = int(np.prod(fr))
        w = (n + 1) // 2
        v = self.ap[:, self.off:self.off + w].bitcast(BF16)[:, 0:n]
        self.off += w
        assert self.off <= self.cap, (self.off, self.cap)
        return self._shape(v, fr)

    @staticmethod
    def _shape(v, fr):
        if len(fr) == 1:
            return v
        if len(fr) == 2:
            return v.rearrange("p (a b) -> p a b", b=fr[1])
        return v.rearrange("p (a b c) -> p a b c", b=fr[1], c=fr[2])


def build_program():
    nc = bass.Bass("TRN2", target_bir_lowering=False)
    P = Prog()

    def din(name, shape, dt=F32):
        return nc.dram_tensor(name, list(shape), dt, kind="ExternalInput").ap()

    def dout(name, shape, dt=F32):
        return nc.dram_tensor(name, list(shape), dt, kind="ExternalOutput").ap()

    x_in = din("x_in", [NT, D])
    ck_in = din("ck", [PAST, 512])
    cv_in = din("cv", [PAST, 512])
    wts = {}
    for f in ("ffn1", "ffn2"):
        wts[f] = (din(f + "_wg", [D, FF]), din(f + "_wu", [D, FF]), din(f + "_wd", [FF, D]))
    w_in = din("w_in", [D, 2560])
    w_out = din("w_out", [D, D])
    cols_in = din("cols", [128, NCOL])
    lng_in = din("lng_b", [128, 512])
    lnb_in = din("lnb_b", [128, 512])
    wsT_in = din("wsT", [128, 8, 128])
    mskT_in = din("mskT", [128, 128])
    bsT_in = din("bsT", [128, 4, 128])
    ident_in = din("ident", [128, 128])
    tm_in = din("tm", [128, 128])
    ngm_in = din("ngm", [128, 128])

    y_out = dout("y", [NT, D])
    k_out = dout("k_o", [NT, 512])
    v_out = dout("v_o", [NT, 512])
    g_out = dout("g_o", [NS, 512])

    sndKt = nc.dram_tensor("sndK", [512, 2048], BF16)
    sndVt = nc.dram_tensor("sndV", [512, 2048], BF16)
    rcvKt = nc.dram_tensor("rcvK", [1024, 2048], BF16)
    rcvVt = nc.dram_tensor("rcvV", [1024, 2048], BF16)
    sndK = sndKt.ap().rearrange("(h p) t -> p h t", p=128)
    sndV = sndVt.ap().rearrange("r (a c) -> (r a) c", c=512)
    rcvK = rcvKt.ap()[0:512, :].rearrange("(h p) t -> p h t", p=128)
    rcvV = rcvVt.ap()[0:512, :].rearrange("r (a c) -> (r a) c", c=512)

    es = ExitStack()
    with es:
        def sb(name, shape, dt):
            return es.enter_context(nc.sbuf_tensor("sb_" + name, list(shape), dt))

        xT = sb("xT", [128, DC, NT], F32)
        cols = sb("cols", [128, NCOL], F32)
        identf = sb("identf", [128, 128], F32)
        identb = sb("identb", [128, 128], BF16)
        onesb = sb("onesb", [128, 128], BF16)
        negonesb = sb("negonesb", [128, 128], BF16)
        tmb = sb("tmb", [128, 128], BF16)
        ngmb = sb("ngmb", [128, 128], BF16)
        wsTm = sb("wsTm", [128, 8, 128], BF16)
        bsT = sb("bsT", [128, 4, 128], F32)
        lngb = sb("lngb", [128, 512], F32)
        lnbb = sb("lnbb", [128, 512], F32)
        onecol = sb("onecol", [128, 1], F32)
        epscol = sb("epscol", [128, 1], F32)
        KTs = sb("KTs", [128, 4, 128], BF16)
        Vs = sb("Vs", [128, 512], BF16)
        AW = 32256
        arena_t = sb("arena", [128, AW], F32)
        A = Arena(arena_t[:, :])
        psum = [es.enter_context(nc.psum_tensor(f"ps{i}", [128, 512], F32)) for i in range(8)]
        psb = [Buf() for _ in range(8)]

        B_xT = [Buf() for _ in SGS]
        B_const = Buf()

        def sg_index(c0):
            return [i for i, (a, n) in enumerate(SGS) if a == c0][0]

        C_F1PRE, C_F1POST, C_MPRE, C_MPOST, C_F2PRE, C_F2POST, C_GSB, C_GSGU, C_FLAG = 0, 8, 16, 24, 32, 40, 48, 52, 56

        A.reset()
        t_f = A.f32(128, 8, 128)
        t_m = A.f32(128, 128)
        t_t = A.f32(128, 128)
        t_n = A.f32(128, 128)
        B_t = Buf()
        P.dma(lambda e: e.dma_start(out=cols[:], in_=cols_in), "c")
        P.dma(lambda e: e.dma_start(out=identf[:], in_=ident_in), "c")
        P.dma(lambda e: e.dma_start(out=bsT[:], in_=bsT_in), "c")
        P.dma(lambda e: e.dma_start(out=lngb[:], in_=lng_in), "c")
        P.dma(lambda e: e.dma_start(out=lnbb[:], in_=lnb_in), "c")
        P.dma(lambda e: e.dma_start(out=t_f, in_=wsT_in), "c")
        P.dma(lambda e: e.dma_start(out=t_m, in_=mskT_in), "c")
        P.dma(lambda e: e.dma_start(out=t_t, in_=tm_in), "c")
        P.dma(lambda e: e.dma_start(out=t_n, in_=ngm_in), "c")
        B_const.w = B_t.w = ("d_c", P.cnt["d_c"])
        P.op("pool", lambda e: e.memset(onesb[:], 1.0), writes=[B_const])
        P.op("pool", lambda e: e.memset(KTs[:], 0.0), writes=[B_const])
        P.op("pool", lambda e: e.memset(Vs[:], 0.0), writes=[B_const])
        P.op("pool", lambda e: e.memset(negonesb[:], -1.0), writes=[B_const])
        P.op("pool", lambda e: e.memset(onecol[:], 1.0), writes=[B_const])
        P.op("pool", lambda e: e.memset(epscol[:], EPS), writes=[B_const])
        P.op("dve", lambda e: e.tensor_copy(out=identb[:], in_=identf[:]), reads=[B_const], writes=[B_const])
        P.op("dve", lambda e: e.tensor_copy(out=tmb[:], in_=t_t), reads=[B_t], writes=[B_const])
        P.op("dve", lambda e: e.tensor_copy(out=ngmb[:], in_=t_n), reads=[B_t], writes=[B_const])
        for g in range(8):
            P.op("dve", lambda e, g=g: e.tensor_tensor(out=wsTm[:, g, :], in0=t_f[:, g, :], in1=t_m, op=ALU.mult),
                 reads=[B_t], writes=[B_const])
        for c in (C_F1POST, C_F2POST):
            P.op("dve", lambda e, c=c: e.tensor_scalar(out=cols[:, c:c + 8], in0=cols[:, c:c + 8], scalar1=0.5,
                                                       scalar2=None, op0=ALU.mult), reads=[B_const], writes=[B_const])
        P.barrier()

        A.reset()
        xst = [A.f32(128, D) for _ in range(4)]
        B_xst = [Buf() for _ in range(4)]
        tiles = [(t * 128, 128) for t in range(16)] + [(NP, NS)]
        for ti, (r0, rows) in enumerate(tiles):
            s = ti % 4
            P.dma(lambda e, s=s, r0=r0, rows=rows: e.dma_start(out=xst[s][0:rows, :], in_=x_in[r0:r0 + rows, :]),
                  "x%d" % s, writes=[B_xst[s]])
            for hb in range(2):
                bank = (2 * ti + hb) % 4

                def tr(e, s=s, hb=hb, bank=bank, rows=rows):
                    ins = None
                    for q in range(4):
                        dc = hb * 4 + q
                        ins = e.transpose(out=psum[bank][:, q * 128:q * 128 + rows],
                                          in_=xst[s][0:rows, dc * 128:(dc + 1) * 128],
                                          identity=identf[0:rows, 0:rows])
                    return ins
                P.op("pe", tr, reads=[B_xst[s], B_const], writes=[psb[bank]])
                sgi = min(r0 // 512, 4)
                eng = "dve" if hb == 0 else "act"

                def ev(e, hb=hb, bank=bank, rows=rows, r0=r0, eng=eng):
                    src = psum[bank][:, :].rearrange("p (q c) -> p q c", c=128)[:, :, 0:rows]
                    dst = xT[:, hb * 4:hb * 4 + 4, r0:r0 + rows]
                    if eng == "dve":
                        return e.tensor_copy(out=dst, in_=src)
                    return e.activation(out=dst, in_=src, func=AF.Copy)
                P.op(eng, ev, reads=[psb[bank]], writes=[B_xT[sgi]])
        P.barrier()

        def rms_rstd(src_sq_fn, nchunks, n, ps_bank, rstd_ap, B_rstd, reads, sq_ap, B_sq, scale):
            src_sq_fn()
            P.op("pe", lambda e: _mm_chain(e, psum[ps_bank][:, 0:n], [(onesb[:, :], sq_ap[:, c, 0:n]) for c in range(nchunks)]),
                 reads=[B_sq, B_const], writes=[psb[ps_bank]])
            P.op("act", lambda e: e.activation(out=rstd_ap, in_=psum[ps_bank][:, 0:n], func=AF.Ln, bias=epscol[:], scale=scale),
                 reads=[psb[ps_bank], B_const], writes=[B_rstd])
            P.op("act", lambda e: e.activation(out=rstd_ap, in_=rstd_ap, func=AF.Exp, scale=-0.5),
                 reads=[B_rstd], writes=[B_rstd])

        def _mm_chain(e, out, pairs, tp=None, first=True, last=True):
            ins = None
            n = len(pairs)
            for i, pr in enumerate(pairs):
                l, r = pr[0], pr[1]
                kw = {}
                if len(pr) > 2:
                    kw["tile_position"] = pr[2]
                elif tp is not None:
                    kw["tile_position"] = tp
                ins = e.matmul(out, lhsT=l, rhs=r, start=(first and i == 0), stop=(last and i == n - 1), **kw)
            return ins

        def prenorm(gstart, sgs, gcol, hT, B_hT, sq, B_sq, rstd, B_rstd):
            for si, (c0, n) in enumerate(sgs):
                lc = c0 - gstart
                bx = B_xT[sg_index(c0)]
                s2 = si % 2

                def sqf(c0=c0, n=n, s2=s2, bx=bx):
                    P.op("pool", lambda e: e.tensor_tensor(out=sq[s2][:, :, 0:n], in0=xT[:, :, c0:c0 + n],
                                                           in1=xT[:, :, c0:c0 + n], op=ALU.mult),
                         reads=[bx], writes=[B_sq[s2]])
                rms_rstd(sqf, DC, n, 7, rstd[s2][:, 0:n], B_rstd[s2], [bx], sq[s2], B_sq[s2], 1.0 / D)
                for dc in range(DC):
                    P.op("dve", lambda e, dc=dc, c0=c0, n=n, lc=lc, s2=s2: e.scalar_tensor_tensor(
                        out=hT[:, dc, lc:lc + n], in0=xT[:, dc, c0:c0 + n], scalar=cols[:, gcol + dc:gcol + dc + 1],
                        in1=rstd[s2][:, 0:n], op0=ALU.mult, op1=ALU.mult),
                        reads=[bx, B_rstd[s2], B_const], writes=[B_hT[si]])

        def postnorm_residual(gstart, sgs, gcol, ysb, B_ysb, sq, B_sq, rstd, B_rstd, ss_banks=None):
            for si, (c0, n) in enumerate(sgs):
                lc = c0 - gstart
                bx = B_xT[sg_index(c0)]
                s2 = si % 2

                def sqf(lc=lc, n=n, s2=s2, si=si):
                    P.op("pool", lambda e: e.tensor_tensor(out=sq[s2][:, :, 0:n], in0=ysb[:, :, lc:lc + n],
                                                           in1=ysb[:, :, lc:lc + n], op=ALU.mult),
                         reads=[B_ysb[si]], writes=[B_sq[s2]])
                if ss_banks is None:
                    rms_rstd(sqf, DC, n, 7, rstd[s2][:, 0:n], B_rstd[s2], [B_ysb[si]], sq[s2], B_sq[s2], 1.0 / D)
                else:
                    sb_ = ss_banks[si]
                    P.op("act", lambda e, sb_=sb_, n=n, s2=s2: e.activation(out=rstd[s2][:, 0:n], in_=psum[sb_][:, 0:n], func=AF.Ln,
                                                                         bias=epscol[:], scale=1.0 / D),
                         reads=[psb[sb_], B_const], writes=[B_rstd[s2]])
                    P.op("act", lambda e, n=n, s2=s2: e.activation(out=rstd[s2][:, 0:n], in_=rstd[s2][:, 0:n], func=AF.Exp, scale=-0.5),
                         reads=[B_rstd[s2]], writes=[B_rstd[s2]])
                for dc in range(DC):
                    eng = "dve" if dc % 2 == 0 else "pool"
                    P.op(eng, lambda e, dc=dc, lc=lc, n=n, s2=s2: e.tensor_tensor(
                        out=ysb[:, dc, lc:lc + n], in0=ysb[:, dc, lc:lc + n], in1=rstd[s2][:, 0:n], op=ALU.mult),
                        reads=[B_rstd[s2]], writes=[B_ysb[si]])
                    P.op("dve", lambda e, dc=dc, c0=c0, n=n, lc=lc: e.scalar_tensor_tensor(
                        out=xT[:, dc, c0:c0 + n], in0=ysb[:, dc, lc:lc + n], scalar=cols[:, gcol + dc:gcol + dc + 1],
                        in1=xT[:, dc, c0:c0 + n], op0=ALU.mult, op1=ALU.add),
                        reads=[B_ysb[si], B_const], writes=[bx])

        def ffn(name, c_pre, c_post):
            wg, wu, wd = wts[name]
            wgv = wg.rearrange("(kc p) f -> p kc f", p=128)
            wuv = wu.rearrange("(kc p) f -> p kc f", p=128)
            wdv = wd.rearrange("(fc p) m -> p fc m", p=128)
            for gstart, sgs in GROUPS:
                A.reset()
                off0 = A.off
                hT = A.bf16(128, DC, GW)
                gst32 = [[A.f32(128, DC, 128) for _ in range(2)] for _ in range(2)]
                gstb = [[A.bf16(128, DC, 128) for _ in range(2)] for _ in range(2)]
                end_alias = A.off
                A.reset(off0)
                ysb = A.f32(128, DC, GW)
                assert A.off <= end_alias
                A.reset(end_alias)
                hidT = A.bf16(128, FC, GW)
                dst32 = [A.f32(128, 11, 128) for _ in range(2)]
                dstb = [A.bf16(128, FC, 128) for _ in range(2)]
                sq1 = A.bf16(128, DC, 512)
                sq = [sq1, sq1]
                rstd1 = A.f32(128, 512)
                rstd = [rstd1, rstd1]
                sgt = [A.f32(128, 512) for _ in range(2)]
                B_hT = [Buf() for _ in sgs]
                B_hid = [Buf() for _ in sgs]
                B_ysb = [Buf() for _ in sgs]
                B_sq1 = Buf()
                B_sq = [B_sq1, B_sq1]
                B_rstd1 = Buf()
                B_rstd = [B_rstd1, B_rstd1]
                B_g32 = [[Buf(), Buf()], [Buf(), Buf()]]
                B_gb = [[Buf(), Buf()], [Buf(), Buf()]]
                B_d32 = [Buf(), Buf()]
                B_db = [[Buf(), Buf()], [Buf(), Buf()]]
                B_sgt = [Buf(), Buf()]

                prenorm(gstart, sgs, c_pre, hT, B_hT, sq, B_sq, rstd, B_rstd)

                k = 0
                for fb in range(FC):
                    s = fb % 2
                    P.dma(lambda e, s=s, fb=fb: e.dma_start(out=gst32[s][0], in_=wgv[:, :, fb * 128:(fb + 1) * 128]),
                          "g%d" % s, writes=[B_g32[s][0]])
                    P.dma(lambda e, s=s, fb=fb: e.dma_start(out=gst32[s][1], in_=wuv[:, :, fb * 128:(fb + 1) * 128]),
                          "u%d" % s, writes=[B_g32[s][1]])
                    P.op("pool", lambda e, s=s: e.tensor_copy(out=gstb[s][0], in_=gst32[s][0]), reads=[B_g32[s][0]], writes=[B_gb[s][0]])
                    P.op("act", lambda e, s=s: e.activation(out=gstb[s][1], in_=gst32[s][1], func=AF.Copy), reads=[B_g32[s][1]], writes=[B_gb[s][1]])
                    for si, (c0, n) in enumerate(sgs):
                        lc = c0 - gstart
                        bg, bu = (2 * k) % 6, (2 * k + 1) % 6
                        k += 1
                        P.op("pe", lambda e, s=s, lc=lc, n=n, bg=bg: _mm_chain(
                            e, psum[bg][:, 0:n], [(gstb[s][0][:, kc, :], hT[:, kc, lc:lc + n]) for kc in range(DC)]),
                            reads=[B_gb[s][0], B_hT[si]], writes=[psb[bg]])
                        P.op("pe", lambda e, s=s, lc=lc, n=n, bu=bu: _mm_chain(
                            e, psum[bu][:, 0:n], [(gstb[s][1][:, kc, :], hT[:, kc, lc:lc + n]) for kc in range(DC)]),
                            reads=[B_gb[s][1], B_hT[si]], writes=[psb[bu]])
                        s2 = k % 2
                        P.op("act", lambda e, n=n, bg=bg, s2=s2: e.activation(out=sgt[s2][:, 0:n], in_=psum[bg][:, 0:n], func=AF.Silu),
                             reads=[psb[bg]], writes=[B_sgt[s2]])
                        P.op("dve", lambda e, n=n, bu=bu, s2=s2, fb=fb, lc=lc: e.tensor_tensor(
                            out=hidT[:, fb, lc:lc + n], in0=sgt[s2][:, 0:n], in1=psum[bu][:, 0:n], op=ALU.mult),
                            reads=[B_sgt[s2], psb[bu]], writes=[B_hid[si]])
                P.barrier()
                sqc = [sgt[i].bitcast(BF16)[:, 0:512] for i in range(2)]
                B_sqc = [Buf(), Buf()]

                k = 0
                for dmc in range(DC):
                    s = dmc % 2
                    for hh in range(2):
                        P.dma(lambda e, s=s, dmc=dmc, hh=hh: e.dma_start(
                            out=dst32[hh], in_=wdv[:, hh * 11:(hh + 1) * 11, dmc * 128:(dmc + 1) * 128]),
                            "d%d" % hh, writes=[B_d32[hh]])
                        if hh == 0:
                            P.op("dve", lambda e, s=s, hh=hh: e.tensor_copy(out=dstb[s][:, hh * 11:(hh + 1) * 11, :], in_=dst32[hh]),
                                 reads=[B_d32[hh]], writes=[B_db[s][hh]])
                        else:
                            P.op("act", lambda e, s=s, hh=hh: e.activation(out=dstb[s][:, hh * 11:(hh + 1) * 11, :], in_=dst32[hh], func=AF.Copy),
                                 reads=[B_d32[hh]], writes=[B_db[s][hh]])
                    for si, (c0, n) in enumerate(sgs):
                        lc = c0 - gstart
                        bank = k % 5
                        t2 = k % 2
                        k += 1
                        P.op("pe", lambda e, s=s, lc=lc, n=n, bank=bank: _mm_chain(
                            e, psum[bank][:, 0:n], [(dstb[s][:, fc, :], hidT[:, fc, lc:lc + n]) for fc in range(FC)]),
                            reads=[B_db[s][0], B_db[s][1], B_hid[si]], writes=[psb[bank]])
                        P.op("act", lambda e, n=n, bank=bank, dmc=dmc, lc=lc: e.activation(
                            out=ysb[:, dmc, lc:lc + n], in_=psum[bank][:, 0:n], func=AF.Copy),
                            reads=[psb[bank]], writes=[B_ysb[si]])
                        P.op("dve", lambda e, n=n, dmc=dmc, lc=lc, t2=t2: e.tensor_tensor(
                            out=sqc[t2][:, 0:n], in0=ysb[:, dmc, lc:lc + n], in1=ysb[:, dmc, lc:lc + n], op=ALU.mult),
                            reads=[B_ysb[si]], writes=[B_sqc[t2]])
                        P.op("pe", lambda e, n=n, si=si, t2=t2, dmc=dmc: e.matmul(
                            psum[5 + si][:, 0:n], lhsT=onesb[:, :], rhs=sqc[t2][:, 0:n], start=(dmc == 0), stop=(dmc == DC - 1)),
                            reads=[B_sqc[t2], B_const], writes=[psb[5 + si]])
                postnorm_residual(gstart, sgs, c_post, ysb, B_ysb, sq, B_sq, rstd, B_rstd, ss_banks=[5 + si for si in range(len(sgs))])
                P.barrier()

        def mix_phase():
            w_inv = w_in.rearrange("(kc p) f -> p kc f", p=128)
            w_outv = w_out.rearrange("(fc p) m -> p fc m", p=128)
            A.reset()
            QT = A.bf16(128, 4, NT)
            QT_WORDS = A.off
            mT = A.bf16(128, 8, NT)
            ssb = A.f32(128, NT)
            keep = A.off
            B_QT, B_mT, B_ssb = Buf(), Buf(), Buf()
            B_snd, B_rcv = Buf(), Buf()
            hTm = A.bf16(128, DC, 512)
            wstb = [A.bf16(128, DC, 256) for _ in range(2)]
            uT = A.f32(128, 4, 512)
            gG = A.f32(128, 4, 512)
            kvtm = A.f32(128, 4, 512)
            kvb = A.bf16(128, 4, 512)
            gnb = A.bf16(128, 4, 512)
            sq1 = A.bf16(128, DC, 512)
            rstd1 = A.f32(128, 512)
            stat = A.f32(128, 16)
            p1_end = A.off
            A.reset(QT_WORDS)
            wst32 = [A.f32(128, DC, 256) for _ in range(2)]
            assert A.off <= QT_WORDS + 4224
            A.reset(QT_WORDS + 8448)
            osgG = A.f32(128, 4, 512)
            assert A.off <= keep
            A.reset(p1_end)
            B_hTm, B_w32, B_wb = [Buf()], [Buf(), Buf()], [Buf(), Buf()]
            B_uT, B_gG, B_kvtm, B_kvb, B_gnb, B_sq, B_rstd, B_stat, B_osg = Buf(), Buf(), Buf(), Buf(), Buf(), Buf(), Buf(), Buf(), Buf()
            pk = [0]
            wk = [0]

            def bank6():
                pk[0] += 1
                return pk[0] % 6

            for gi, (c0, n) in enumerate(SGS):
                samp = (c0 == NP)
                gt = [(0, 64)] if samp else [(t * 128, 128) for t in range(4)]
                nt_ = len(gt)
                rows = gt[0][1]
                prenorm(c0, [(c0, n)], C_MPRE, hTm, B_hTm, [sq1, sq1], [B_sq, B_sq], [rstd1, rstd1], [B_rstd, B_rstd])

                def do_block(blk, c0=c0, n=n, samp=samp, gt=gt):
                    s = wk[0] % 2
                    wk[0] += 1
                    P.dma(lambda e: e.dma_start(out=wst32[s], in_=w_inv[:, :, blk * 256:(blk + 1) * 256]), "wi%d" % s, writes=[B_w32[s]])
                    P.op("act", lambda e: e.activation(out=wstb[s], in_=wst32[s], func=AF.Copy), reads=[B_w32[s]], writes=[B_wb[s]])
                    kind = blk // 2
                    half = blk % 2
                    if kind in (0, 1, 3):
                        for j in range(2):
                            hp = half * 2 + j
                            bk = bank6()
                            P.op("pe", lambda e, j=j, bk=bk: _mm_chain(
                                e, psum[bk][:, 0:n], [(wstb[s][:, kc, j * 128:(j + 1) * 128], hTm[:, kc, 0:n]) for kc in range(DC)]),
                                reads=[B_wb[s], B_hTm[0]], writes=[psb[bk]])
                            if kind == 0:
                                P.op("dve", lambda e, hp=hp, bk=bk: e.tensor_scalar(
                                    out=QT[:, hp, c0:c0 + n], in0=psum[bk][:, 0:n], scalar1=0.125, scalar2=None, op0=ALU.mult),
                                    reads=[psb[bk]], writes=[B_QT])
                            elif kind == 1:
                                dst = KTs[:, hp, 0:NS] if samp else kvb[:, hp, 0:n]
                                P.op("dve", lambda e, dst=dst, bk=bk: e.tensor_copy(out=dst, in_=psum[bk][:, 0:n]),
                                     reads=[psb[bk]], writes=[B_const if samp else B_kvb])
                            else:
                                P.op("act", lambda e, hp=hp, bk=bk: e.activation(out=uT[:, hp, 0:n], in_=psum[bk][:, 0:n], func=AF.Gelu),
                                     reads=[psb[bk]], writes=[B_uT])
                        if kind == 1 and half == 1 and not samp:
                            P.dma(lambda e: e.dma_start(out=sndK[:, :, c0:c0 + n], in_=kvb[:, :, 0:n]), "sk",
                                  reads=[B_kvb], writes=[B_snd])
                    if kind in (1, 2, 4):
                        for ti, (r0, rows_) in enumerate(gt):
                            bk = bank6()
                            P.op("pe", lambda e, bk=bk, r0=r0, rows_=rows_: _mm_chain(
                                e, psum[bk][0:rows_, 0:256], [(hTm[:, kc, r0:r0 + rows_], wstb[s][:, kc, :]) for kc in range(DC)]),
                                reads=[B_wb[s], B_hTm[0]], writes=[psb[bk]])
                            cs = slice(half * 256, half * 256 + 256)
                            if kind == 4:
                                P.op("act", lambda e, bk=bk, ti=ti, rows_=rows_, cs=cs: e.activation(
                                    out=gG[0:rows_, ti, cs], in_=psum[bk][0:rows_, 0:256], func=AF.Gelu),
                                    reads=[psb[bk]], writes=[B_gG])
                            else:
                                P.op("dve", lambda e, bk=bk, ti=ti, rows_=rows_, cs=cs: e.tensor_copy(
                                    out=kvtm[0:rows_, ti, cs], in_=psum[bk][0:rows_, 0:256]),
                                    reads=[psb[bk]], writes=[B_kvtm])
                        if kind in (1, 2) and half == 1:
                            dsto = k_out if kind == 1 else v_out
                            if samp:
                                P.dma(lambda e: e.dma_start(out=dsto[NP:NT, :], in_=kvtm[0:NS, 0, :]), "okv", reads=[B_kvtm])
                            else:
                                P.dma(lambda e: e.dma_start(
                                    out=dsto[c0:c0 + 512, :].rearrange("(t p) c -> p t c", p=128), in_=kvtm[:, :, :]), "okv", reads=[B_kvtm])
                            if kind == 2:
                                if samp:
                                    P.op("pool", lambda e: e.tensor_copy(out=Vs[0:NS, :], in_=kvtm[0:NS, 0, :]), reads=[B_kvtm], writes=[B_const])
                                else:
                                    P.op("pool", lambda e: e.tensor_copy(out=kvb[:, :, :], in_=kvtm[:, :, :]), reads=[B_kvtm], writes=[B_kvb])
                                    P.dma(lambda e: e.dma_start(
                                        out=sndV[c0:c0 + 512, :].rearrange("(t p) c -> p t c", p=128), in_=kvb[:, :, :]), "sv",
                                        reads=[B_kvb], writes=[B_snd])

                def tail1(rows=rows, nt_=nt_, samp=samp):
                    for ti in range(nt_):
                        P.op("dve", lambda e, ti=ti: e.reduce_sum(out=stat[0:rows, ti:ti + 1], in_=gG[0:rows, ti, :], axis=AX.X),
                             reads=[B_gG], writes=[B_stat])
                    P.op("dve", lambda e: e.tensor_scalar(out=stat[0:rows, 0:nt_], in0=stat[0:rows, 0:nt_],
                                                          scalar1=1.0 / 512, scalar2=None, op0=ALU.mult),
                         reads=[B_stat], writes=[B_stat])
                    for ti in range(nt_):
                        P.op("dve", lambda e, ti=ti: e.tensor_scalar(out=gG[0:rows, ti, :], in0=gG[0:rows, ti, :],
                                                                     scalar1=stat[0:rows, ti:ti + 1], scalar2=None, op0=ALU.subtract),
                             reads=[B_stat], writes=[B_gG])
                        P.op("pool", lambda e, ti=ti: e.tensor_tensor(out=osgG[0:rows, ti, :], in0=gG[0:rows, ti, :],
                                                                      in1=gG[0:rows, ti, :], op=ALU.mult),
                             reads=[B_gG], writes=[B_osg])
                        P.op("dve", lambda e, ti=ti: e.reduce_sum(out=stat[0:rows, 4 + ti:5 + ti], in_=osgG[0:rows, ti, :], axis=AX.X),
                             reads=[B_osg], writes=[B_stat])
                    P.op("act", lambda e: e.activation(out=stat[0:rows, 8:8 + nt_], in_=stat[0:rows, 4:4 + nt_], func=AF.Ln,
                                                       bias=epscol[0:rows, :], scale=1.0 / 512),
                         reads=[B_stat, B_const], writes=[B_stat])
                    P.op("act", lambda e: e.activation(out=stat[0:rows, 8:8 + nt_], in_=stat[0:rows, 8:8 + nt_], func=AF.Exp, scale=-0.5),
                         reads=[B_stat], writes=[B_stat])
                    for ti in range(nt_):
                        P.op("dve", lambda e, ti=ti: e.scalar_tensor_tensor(
                            out=gG[0:rows, ti, :], in0=gG[0:rows, ti, :], scalar=stat[0:rows, 8 + ti:9 + ti], in1=lngb[0:rows, :],
                            op0=ALU.mult, op1=ALU.mult), reads=[B_stat, B_const], writes=[B_gG])
                        P.op("pool", lambda e, ti=ti: e.tensor_tensor(out=gG[0:rows, ti, :], in0=gG[0:rows, ti, :],
                                                                      in1=lnbb[0:rows, :], op=ALU.add),
                             reads=[B_const], writes=[B_gG])
                        P.op("pool", lambda e, ti=ti: e.tensor_copy(out=gnb[0:rows, ti, :], in_=gG[0:rows, ti, :]),
                             reads=[B_gG], writes=[B_gnb])
                    if samp:
                        P.dma(lambda e: e.dma_start(out=g_out[:, :], in_=gG[0:NS, 0, :]), "og", reads=[B_gG])

                def tail2(rows=rows, nt_=nt_, n=n, c0=c0):
                    for ti in range(nt_):
                        bk = bank6()

                        def sgu_mm(e, ti=ti, bk=bk):
                            ins = None
                            for gp in range(4):
                                for gl in range(2):
                                    g = 2 * gp + gl
                                    ins = e.matmul(psum[bk][gl * 64:(gl + 1) * 64, gp * 128:gp * 128 + rows],
                                                   lhsT=gnb[0:rows, ti, g * 64:(g + 1) * 64], rhs=wsTm[0:rows, g, 0:rows],
                                                   start=True, stop=True, tile_position=(0, gl * 64))
                            return ins
                        P.op("pe", sgu_mm, reads=[B_gnb, B_const], writes=[psb[bk]])
                        r0 = ti * 128
                        P.op("dve", lambda e, bk=bk, r0=r0: e.tensor_tensor(
                            out=osgG[:, :, r0:r0 + rows], in0=psum[bk][:, :].rearrange("p (a b) -> p a b", b=128)[:, :, 0:rows],
                            in1=bsT[:, :, 0:rows], op=ALU.add), reads=[psb[bk], B_const], writes=[B_osg])
                        P.op("pool", lambda e, r0=r0: e.tensor_tensor(
                            out=osgG[:, :, r0:r0 + rows], in0=osgG[:, :, r0:r0 + rows], in1=uT[:, :, r0:r0 + rows], op=ALU.mult),
                            reads=[B_uT], writes=[B_osg])

                    def sqf():
                        P.op("pool", lambda e: e.tensor_tensor(out=sq1[:, 0:4, 0:n], in0=osgG[:, :, 0:n], in1=osgG[:, :, 0:n], op=ALU.mult),
                             reads=[B_osg], writes=[B_sq])
                    rms_rstd(sqf, 4, n, 7, rstd1[:, 0:n], B_rstd, [B_osg], sq1, B_sq, 1.0 / 512)
                    for gp in range(4):
                        P.op("dve", lambda e, gp=gp: e.scalar_tensor_tensor(
                            out=mT[:, 4 + gp, c0:c0 + n], in0=osgG[:, gp, 0:n], scalar=cols[:, C_GSGU + gp:C_GSGU + gp + 1],
                            in1=rstd1[:, 0:n], op0=ALU.mult, op1=ALU.mult), reads=[B_osg, B_rstd, B_const], writes=[B_mT])

                for blk in (0, 1, 6, 7, 8, 9):
                    do_block(blk)
                tail1()
                for blk in (2, 3, 4, 5):
                    do_block(blk)
                tail2()
            if MIX_STOP <= 1:
                P.barrier()
                return
            for st_, rt_ in (() if NO_CC else ((sndKt, rcvKt), (sndVt, rcvVt))):
                P.op("pool", lambda e, st_=st_, rt_=rt_: e.collective_compute(
                    "AllGather", ALU.bypass, replica_groups=[[0, 1], [2, 3], [4, 5], [6, 7]], ins=[st_.ap()], outs=[rt_.ap()]),
                    reads=[B_snd], writes=[B_rcv], sem="cc", inc=1)
            if NO_CC:
                for st_, rt_ in ((sndKt, rcvKt), (sndVt, rcvVt)):
                    P.dma(lambda e, st_=st_, rt_=rt_: e.dma_start(out=rt_.ap()[0:512, :], in_=st_.ap()), "dbgcc",
                          reads=[B_snd], writes=[B_rcv])
            P.barrier()

            if MIX_STOP <= 2:
                return
            A.reset(keep)
            kvreg = A.off
            KTc = A.bf16(128, 4, PAST)
            Vc = A.bf16(128, 16, 512)
            cst32 = [A.f32(128, 512) for _ in range(2)]
            cstb = [A.bf16(128, 512) for _ in range(2)]
            A.reset(kvreg)
            KThp = [A.bf16(128, 4096) for _ in range(2)]
            Vhp = [A.bf16(128, 32, 128) for _ in range(2)]
            A.reset(kvreg + 9728)
            e32 = [A.f32(128, 512) for _ in range(2)]
            Lp = [A.bf16(128, 512) for _ in range(3)]
            At = [A.bf16(128, 512) for _ in range(2)]
            Ls32 = [A.f32(128, 512) for _ in range(2)]
            LsB = [A.bf16(128, 512) for _ in range(3)]
            osq = A.bf16(128, 512)
            otmp = A.f32(128, 512)
            B_otmp = Buf()
            QTm = [A.bf16(128, 512) for _ in range(2)]
            QTs8 = A.bf16(128, 8, NS)
            B_QTm, B_QTs8 = [Buf(), Buf()], Buf()
            B_KTc, B_Vc, B_c32, B_cb = Buf(), Buf(), [Buf(), Buf()], [Buf(), Buf()]
            B_KThp, B_Vhp = [Buf(), Buf()], [Buf(), Buf()]
            B_e32, B_Lp, B_At, B_Ls32, B_LsB, B_osq = [Buf(), Buf()], [Buf(), Buf(), Buf()], [Buf(), Buf()], [Buf(), Buf()], [Buf(), Buf(), Buf()], Buf()
            ctr = {"n": 0, "l": 0, "o": 0, "q": 0}

            pendB, pendC = [], []

            def attn_step(zfn, Efn, avfn, ks, a0, first, last, carry_bufs, obuf, pre=None, post=None):
                n = ctr["n"]
                ctr["n"] += 1
                zb, Eb, ei, li, ai = n % 2, 2 + n % 2, n % 2, n % 3, n % 2
                cur = ctr["l"] % 2
                nxt = 1 - cur
                if pre is not None:
                    pre()
                if first:
                    P.op("pool", lambda e: e.memset(Ls32[cur][:, :], 0.0), writes=[B_Ls32[cur]])
                P.op("pe", lambda e: zfn(e, psum[zb]), reads=[B_QT, B_const] + carry_bufs, writes=[psb[zb]])
                P.op("act", lambda e: e.activation(out=e32[ei][0:ks, a0:512], in_=psum[zb][0:ks, a0:512], func=AF.Exp),
                     reads=[psb[zb]], writes=[B_e32[ei]])
                if a0 > 0:
                    P.op("pool", lambda e: e.memset(Lp[li][:, 0:a0], 0.0), writes=[B_Lp[li]])
                P.op("act", lambda e: e.activation(out=Lp[li][0:ks, a0:512], in_=e32[ei][0:ks, a0:512], func=AF.Ln, bias=onecol[0:ks, :]),
                     reads=[B_e32[ei], B_const], writes=[B_Lp[li]])
                if not last:
                    P.op("dve", lambda e: e.tensor_tensor(out=Ls32[nxt][:, :], in0=Ls32[cur][:, :], in1=Lp[li][:, :], op=ALU.add),
                         reads=[B_Ls32[cur], B_Lp[li]], writes=[B_Ls32[nxt]])
                    P.op("dve", lambda e: e.tensor_tensor(out=LsB[(n + 1) % 3][:, :], in0=Ls32[cur][:, :], in1=Lp[li][:, :], op=ALU.add),
                         reads=[B_Ls32[cur], B_Lp[li]], writes=[B_LsB[(n + 1) % 3]])
                    ctr["l"] += 1

                def partB():
                    rd = [B_QT, B_const, B_Lp[li]] + carry_bufs + ([] if first else [B_LsB[n % 3]])
                    P.op("pe", lambda e: Efn(e, psum[Eb], Lp[li], None if first else LsB[n % 3]), reads=rd, writes=[psb[Eb]])
                    if a0 > 0:
                        P.op("pool", lambda e: e.memset(At[ai][:, 0:a0], 0.0), writes=[B_At[ai]])
                    P.op("act", lambda e: e.activation(out=At[ai][0:ks, a0:512], in_=psum[Eb][0:ks, a0:512], func=AF.Exp),
                         reads=[psb[Eb]], writes=[B_At[ai]])

                def partC():
                    P.op("pe", lambda e: avfn(e, At[ai], first, last), reads=[B_At[ai]] + carry_bufs, writes=[obuf])
                    if post is not None:
                        post()
                if pendB:
                    pb, pc = pendB.pop(0)
                    pb()
                    if pendC:
                        pendC.pop(0)()
                    pendC.append(pc)
                pendB.append((partB, partC))

            def attn_flush():
                while pendB:
                    pb, pc = pendB.pop(0)
                    pb()
                    if pendC:
                        pendC.pop(0)()
                    pendC.append(pc)
                while pendC:
                    pendC.pop(0)()

            if OVERLAP_CC:
                P.base.pop("cc", None)
            for blk in range(16):
                s = blk % 2
                P.dma(lambda e, s=s, blk=blk: e.dma_start(out=cst32[s], in_=ck_in[blk * 128:(blk + 1) * 128, :]), "ck%d" % s, writes=[B_c32[s]])
                P.op("act", lambda e, s=s: e.activation(out=cstb[s], in_=cst32[s], func=AF.Copy), reads=[B_c32[s]], writes=[B_cb[s]])
                bk = 6 + blk % 2

                def trc(e, bk=bk, s=s):
                    ins = None
                    pv = psum[bk][:, :].bitcast(BF16)
                    for hp in range(4):
                        ins = e.transpose(out=pv[:, hp * 128:(hp + 1) * 128], in_=cstb[s][:, hp * 128:(hp + 1) * 128], identity=identb[:, :])
                    return ins
                P.op("pe", trc, reads=[B_cb[s], B_const], writes=[psb[bk]])
                P.op("dve", lambda e, bk=bk, blk=blk: e.tensor_copy(
                    out=KTc[:, :, blk * 128:(blk + 1) * 128],
                    in_=psum[bk][:, :].bitcast(BF16)[:, 0:512].rearrange("p (a b) -> p a b", b=128)),
                    reads=[psb[bk]], writes=[B_KTc])
                s2 = (blk + 1) % 2
                P.dma(lambda e, s2=s2, blk=blk: e.dma_start(out=cst32[s2], in_=cv_in[blk * 128:(blk + 1) * 128, :]), "ck%d" % s2, writes=[B_c32[s2]])
                P.op("act", lambda e, s2=s2, blk=blk: e.activation(out=Vc[:, blk, :], in_=cst32[s2], func=AF.Copy), reads=[B_c32[s2]], writes=[B_Vc])

            QS = slice(NP, NT)
            P.op("pool", lambda e: e.memset(QTs8[:, :, :], 0.0), writes=[B_QTs8])
            for h in range(8):
                P.op("pool", lambda e, h=h: e.tensor_copy(out=QTs8[(h % 2) * 64:(h % 2 + 1) * 64, h, :],
                                                          in_=QT[(h % 2) * 64:(h % 2 + 1) * 64, h // 2, NP:NT]),
                     reads=[B_QT], writes=[B_QTs8])
            def sample_epilogue():
                for hp in range(4):
                    P.op("dve", lambda e, hp=hp: e.tensor_scalar(out=mT[:, hp, QS], in0=psum[4][:, hp * 128:hp * 128 + 64],
                                                                 scalar1=cols[:, C_GSB + hp:C_GSB + hp + 1], scalar2=None, op0=ALU.mult),
                         reads=[psb[4], B_const], writes=[B_mT])
                for hp in range(4):
                    P.op("dve", lambda e, hp=hp: e.tensor_copy(out=otmp[:, hp * 64:(hp + 1) * 64], in_=psum[4][:, hp * 128:hp * 128 + 64]),
                         reads=[psb[4]], writes=[B_otmp])
                P.op("pool", lambda e: e.tensor_tensor(out=osq[:, 0:256], in0=otmp[:, 0:256], in1=otmp[:, 0:256], op=ALU.mult),
                     reads=[B_otmp], writes=[B_osq])
                P.op("pe", lambda e: _mm_chain(e, psum[6][:, 0:NS], [(onesb[:, :], osq[:, hp * 64:(hp + 1) * 64]) for hp in range(4)]),
                     reads=[B_osq, B_const], writes=[psb[6]])
                P.op("dve", lambda e: e.tensor_copy(out=ssb[:, QS], in_=psum[6][:, 0:NS]), reads=[psb[6]], writes=[B_ssb])

            sblocks = [("new", None)] + [("c", b) for b in range(15, -1, -1)]
            for si, (kind, b) in enumerate(sblocks):
                ks = 128
                first, last = si == 0, si == len(sblocks) - 1

                def ksrc(h, kind=kind, b=b):
                    hp, hh = h // 2, h % 2
                    if kind == "new":
                        return KTs[:, hp, :]
                    return KTc[:, hp, b * 128:(b + 1) * 128]

                def vsrc(h, kind=kind, b=b):
                    if kind == "new":
                        return Vs[:, h * 64:(h + 1) * 64]
                    return Vc[:, b, h * 64:(h + 1) * 64]

                def zfn(e, ps, ks=ks, kind=kind, ksrc=ksrc):
                    ins = None
                    for h in range(8):
                        hp, hh = h // 2, h % 2
                        o = ps[0:ks, h * 64:(h + 1) * 64]
                        ins = e.matmul(o, lhsT=ksrc(h), rhs=QTs8[:, h, :], start=True, stop=(kind != "new"))
                        if kind == "new":
                            ins = e.matmul(o, lhsT=identb[:, :], rhs=ngmb[:, 0:NS], start=False, stop=True)
                    return ins

                def Efn(e, ps, Lpt, Lsb, ks=ks, kind=kind, ksrc=ksrc):
                    ins = None
                    for h in range(8):
                        hp, hh = h // 2, h % 2
                        hc = slice(h * 64, (h + 1) * 64)
                        o = ps[0:ks, hc]
                        prs = [(ksrc(h), QTs8[:, h, :]), (tmb[0:ks, 0:ks], Lpt[0:ks, hc])]
                        if Lsb is not None:
                            prs.append((negonesb[:, 0:ks], Lsb[:, hc]))
                        if kind == "new":
                            prs.append((identb[:, :], ngmb[:, 0:NS]))
                        ins = _mm_chain(e, o, prs)
                    return ins

                def avfn(e, Att, first, last, ks=ks, vsrc=vsrc):
                    ins = None
                    for h in range(8):
                        hp, hh = h // 2, h % 2
                        ins = e.matmul(psum[4][hh * 64:(hh + 1) * 64, hp * 128:hp * 128 + 64], lhsT=vsrc(h),
                                       rhs=Att[0:ks, h * 64:(h + 1) * 64], start=(first and hp == 0), stop=(last and hp == 3),
                                       tile_position=(0, hh * 64))
                    return ins
                attn_step(zfn, Efn, avfn, ks, 0, first, last, [B_KTc, B_Vc, B_QTs8], psb[4], post=(sample_epilogue if last else None))
            attn_flush()
            ctr["o"] += 1
            P.barrier()

            if MIX_STOP <= 3:
                return
            rcvVv = rcvV.rearrange("(b p) c -> p b c", p=128)
            sndVv = sndV.rearrange("(b p) c -> p b c", p=128)
            for hp in range(4):
                s = hp % 2
                P.dma(lambda e, s=s, hp=hp: e.dma_start(out=KThp[s][:, 0:NP], in_=rcvK[:, hp, :]), "kt%d" % s, reads=[B_rcv], writes=[B_KThp[s]])
                P.dma(lambda e, s=s, hp=hp: e.dma_start(out=KThp[s][:, NP:2 * NP], in_=sndK[:, hp, :]), "kt%d" % s, reads=[B_snd], writes=[B_KThp[s]])
                P.dma(lambda e, s=s, hp=hp: e.dma_start(out=Vhp[s][:, 0:16, :], in_=rcvVv[:, :, hp * 128:(hp + 1) * 128]), "vt%d" % s,
                      reads=[B_rcv], writes=[B_Vhp[s]])
                P.dma(lambda e, s=s, hp=hp: e.dma_start(out=Vhp[s][:, 16:32, :], in_=sndVv[:, :, hp * 128:(hp + 1) * 128]), "vt%d" % s,
                      reads=[B_snd], writes=[B_Vhp[s]])
                P.op("pool", lambda e, s=s: e.tensor_scalar(out=Vhp[s][:, 0:16, :], in0=Vhp[s][:, 0:16, :], scalar1=cols[:, C_FLAG:C_FLAG + 1],
                                                            scalar2=None, op0=ALU.mult), reads=[B_const], writes=[B_Vhp[s]])
                for j in range(4):
                    q0 = j * 512
                    ob = 4 + ctr["o"] % 2
                    def epilogue(hp=hp, q0=q0, ob=ob):
                        P.op("dve", lambda e: e.tensor_scalar(
                            out=mT[:, hp, q0:q0 + 512], in0=psum[ob][:, :], scalar1=cols[:, C_GSB + hp:C_GSB + hp + 1], scalar2=None, op0=ALU.mult),
                            reads=[psb[ob], B_const], writes=[B_mT])
                        P.op("dve", lambda e: e.tensor_copy(out=otmp[:, :], in_=psum[ob][:, :]), reads=[psb[ob]], writes=[B_otmp])
                        P.op("pool", lambda e: e.tensor_tensor(out=osq[:, :], in0=otmp[:, :], in1=otmp[:, :], op=ALU.mult),
                             reads=[B_otmp], writes=[B_osq])
                        P.op("pe", lambda e: e.matmul(psum[6][:, :], lhsT=onesb[:, :], rhs=osq[:, :], start=True, stop=True),
                             reads=[B_osq, B_const], writes=[psb[6]])
                        if hp == 0:
                            P.op("dve", lambda e: e.tensor_copy(out=ssb[:, q0:q0 + 512], in_=psum[6][:, :]), reads=[psb[6]], writes=[B_ssb])
                        else:
                            P.op("dve", lambda e: e.tensor_tensor(out=ssb[:, q0:q0 + 512], in0=ssb[:, q0:q0 + 512], in1=psum[6][:, :], op=ALU.add),
                                 reads=[psb[6]], writes=[B_ssb])

                    for hh in range(2):
                        rows_ = slice(hh * 64, (hh + 1) * 64)
                        qi = ctr["q"] % 2
                        ctr["q"] += 1
                        qm = QTm[qi]

                        def build_q(qm=qm, hh=hh, rows_=rows_, hp=hp, q0=q0, qi=qi):
                            P.op("pool", lambda e: e.memset(qm[(1 - hh) * 64:(2 - hh) * 64, :], 0.0), writes=[B_QTm[qi]])
                            P.op("pool", lambda e: e.tensor_copy(out=qm[rows_, :], in_=QT[rows_, hp, q0:q0 + 512]),
                                 reads=[B_QT], writes=[B_QTm[qi]])
                        blocks = [(16 + 4 * j + m, m) for m in (3, 2, 1, 0)] + [(bb, None) for bb in range(16 + 4 * j - 1, -1, -1)]
                        for bi, (bb, m) in enumerate(blocks):
                            first, last = bi == 0, bi == len(blocks) - 1
                            a0 = 0 if m is None else 128 * m
                            kl = KThp[s][:, bb * 128:(bb + 1) * 128]
                            if m is None:
                                rngs = [(0, 512, False)]
                            else:
                                rngs = [(a0, a0 + 128, True)] + ([(a0 + 128, 512, False)] if a0 + 128 < 512 else [])

                            def zfn(e, ps, kl=kl, rngs=rngs, qm=qm):
                                ins = None
                                for (x0, x1, msk) in rngs:
                                    prs = [(kl, qm[:, x0:x1])]
                                    if msk:
                                        prs.append((identb[:, :], ngmb[:, :]))
                                    ins = _mm_chain(e, ps[:, x0:x1], prs)
                                return ins

                            def Efn(e, ps, Lpt, Lsb, kl=kl, rngs=rngs, qm=qm):
                                ins = None
                                for (x0, x1, msk) in rngs:
                                    prs = [(kl, qm[:, x0:x1]), (tmb[:, :], Lpt[:, x0:x1])]
                                    if Lsb is not None:
                                        prs.append((negonesb[:, :], Lsb[:, x0:x1]))
                                    if msk:
                                        prs.append((identb[:, :], ngmb[:, :]))
                                    ins = _mm_chain(e, ps[:, x0:x1], prs)
                                return ins

                            def avfn(e, Att, first, last, s=s, bb=bb, rows_=rows_, hh=hh, ob=ob):
                                return e.matmul(psum[ob][rows_, 0:512], lhsT=Vhp[s][:, bb, hh * 64:(hh + 1) * 64], rhs=Att[:, 0:512],
                                                start=first, stop=last, tile_position=(0, hh * 64))
                            attn_step(zfn, Efn, avfn, 128, a0, first, last, [B_KThp[s], B_Vhp[s], B_QTm[qi]], psb[ob],
                                      pre=(build_q if first else None), post=(epilogue if (last and hh == 1) else None))
                    ctr["o"] += 1
            attn_flush()
            P.barrier()

            if MIX_STOP <= 4:
                return
            P.op("act", lambda e: e.activation(out=ssb[:, :], in_=ssb[:, :], func=AF.Ln, bias=epscol[:], scale=1.0 / 512),
                 reads=[B_const], writes=[B_ssb])
            P.op("act", lambda e: e.activation(out=ssb[:, :], in_=ssb[:, :], func=AF.Exp, scale=-0.5), writes=[B_ssb])
            for gstart, sgs in GROUPS:
                A.reset(keep)
                ysb = A.f32(128, DC, GW)
                wo32 = [A.f32(128, DC, 128) for _ in range(2)]
                wob = [A.bf16(128, DC, 128) for _ in range(2)]
                sqo = A.bf16(128, DC, 512)
                rso = A.f32(128, 512)
                tmpo = [A.f32(128, 512) for _ in range(2)]
                B_ysb = [Buf() for _ in sgs]
                B_wo32, B_wob, B_sqo, B_rso, B_tmpo = [Buf(), Buf()], [Buf(), Buf()], Buf(), Buf(), [Buf(), Buf()]
                k = 0
                for dmc in range(DC):
                    s = dmc % 2
                    P.dma(lambda e, s=s, dmc=dmc: e.dma_start(out=wo32[s], in_=w_outv[:, :, dmc * 128:(dmc + 1) * 128]), "wo%d" % s, writes=[B_wo32[s]])
                    P.op("act", lambda e, s=s: e.activation(out=wob[s], in_=wo32[s], func=AF.Copy), reads=[B_wo32[s]], writes=[B_wob[s]])
                    for si, (c0, n) in enumerate(sgs):
                        lc = c0 - gstart
                        ba, bb_ = (2 * k) % 6, (2 * k + 1) % 6
                        t2 = k % 2
                        k += 1
                        P.op("pe", lambda e, s=s, c0=c0, n=n, ba=ba: _mm_chain(
                            e, psum[ba][:, 0:n], [(wob[s][:, fc, :], mT[:, fc, c0:c0 + n]) for fc in range(4)]),
                            reads=[B_wob[s], B_mT], writes=[psb[ba]])
                        P.op("pe", lambda e, s=s, c0=c0, n=n, bb_=bb_: _mm_chain(
                            e, psum[bb_][:, 0:n], [(wob[s][:, fc, :], mT[:, fc, c0:c0 + n]) for fc in range(4, 8)]),
                            reads=[B_wob[s], B_mT], writes=[psb[bb_]])
                        P.op("dve", lambda e, c0=c0, n=n, ba=ba, t2=t2: e.tensor_tensor(
                            out=tmpo[t2][:, 0:n], in0=psum[ba][:, 0:n], in1=ssb[:, c0:c0 + n], op=ALU.mult),
                            reads=[psb[ba], B_ssb], writes=[B_tmpo[t2]])
                        P.op("dve", lambda e, n=n, bb_=bb_, t2=t2, dmc=dmc, lc=lc: e.tensor_tensor(
                            out=ysb[:, dmc, lc:lc + n], in0=tmpo[t2][:, 0:n], in1=psum[bb_][:, 0:n], op=ALU.add),
                            reads=[B_tmpo[t2], psb[bb_]], writes=[B_ysb[si]])
                postnorm_residual(gstart, sgs, C_MPOST, ysb, B_ysb, [sqo, sqo], [B_sqo, B_sqo], [rso, rso], [B_rso, B_rso])
                P.barrier()

        if not SKIP_FFN1:
            ffn("ffn1", C_F1PRE, C_F1POST)

        if STOP_AFTER != "ffn1":
            mix_phase()
            if STOP_AFTER != "mix":
                ffn("ffn2", C_F2PRE, C_F2POST)

        A.reset()
        yst = [A.f32(128, D) for _ in range(4)]
        B_yst = [Buf() for _ in range(4)]
        for ti, (r0, rows) in enumerate(tiles):
            s = ti % 4
            sgi = min(r0 // 512, 4)
            for hb in range(2):
                bank = (2 * ti + hb) % 4

                def tr(e, hb=hb, bank=bank, rows=rows, r0=r0):
                    ins = None
                    for q in range(4):
                        dc = hb * 4 + q
                        ins = e.transpose(out=psum[bank][0:rows, q * 128:(q + 1) * 128],
                                          in_=xT[:, dc, r0:r0 + rows], identity=identf[:, :])
                    return ins
                P.op("pe", tr, reads=[B_xT[sgi], B_const], writes=[psb[bank]])
                eng = "dve" if hb == 0 else "act"

                def ev(e, s=s, hb=hb, bank=bank, rows=rows, eng=eng):
                    dst = yst[s][0:rows, hb * 512:(hb + 1) * 512]
                    if eng == "dve":
                        return e.tensor_copy(out=dst, in_=psum[bank][0:rows, :])
                    return e.activation(out=dst, in_=psum[bank][0:rows, :], func=AF.Copy)
                P.op(eng, ev, reads=[psb[bank]], writes=[B_yst[s]])
            P.dma(lambda e, s=s, r0=r0, rows=rows: e.dma_start(out=y_out[r0:r0 + rows, :], in_=yst[s][0:rows, :]),
                  "oy%d" % s, reads=[B_yst[s]])

        names = sorted(P.cnt.keys())
        sems = {k: es.enter_context(nc.semaphore(k)) for k in names}
        block = es.enter_context(nc.Block())
        P.emit(nc, block, sems)
    return nc


_NC_CACHE = {}


def _consts():
    ident = np.eye(128, dtype=np.float32)
    j = np.arange(128)
    tm = -(j[:, None] >= j[None, :]).astype(np.float32)
    ngm = np.where(j[:, None] >= j[None, :], NEG, 0.0).astype(np.float32)
    mskT = ((j[:, None] // 64) <= (j[None, :] // 64)).astype(np.float32)
    return ident, tm, ngm, mskT


def _col(v, n):
    return np.ascontiguousarray(np.asarray(v, np.float32).reshape(n, 128).T)


def kernel(x_prompt, x_sample, cache_k_sb, cache_v_sb,
           ffn1_pre_g, ffn1_post_g, ffn1_w_gate, ffn1_w_up, ffn1_w_down,
           mix_pre_g, mix_post_g, w_in, sgu_ln_g, sgu_ln_b, sgu_w_s, sgu_b_s,
           g_out_sb, g_out_sgu, w_out,
           ffn2_pre_g, ffn2_post_g, ffn2_w_gate, ffn2_w_up, ffn2_w_down):
    f = lambda a: np.ascontiguousarray(np.asarray(a, dtype=np.float32))
    x_prompt, x_sample = f(x_prompt), f(x_sample)
    ck, cv = f(cache_k_sb), f(cache_v_sb)
    ident, tm, ngm, mskT = _consts()
    shared = {
        "ffn1_wg": f(ffn1_w_gate)[0], "ffn1_wu": f(ffn1_w_up)[0], "ffn1_wd": f(ffn1_w_down)[0],
        "ffn2_wg": f(ffn2_w_gate)[0], "ffn2_wu": f(ffn2_w_up)[0], "ffn2_wd": f(ffn2_w_down)[0],
        "w_in": f(w_in)[0], "w_out": f(w_out)[0],
        "lng_b": np.ascontiguousarray(np.broadcast_to(f(sgu_ln_g)[0][None, :], (128, 512))),
        "lnb_b": np.ascontiguousarray(np.broadcast_to(f(sgu_ln_b)[0][None, :], (128, 512))),
        "wsT": np.ascontiguousarray(f(sgu_w_s)[0].transpose(2, 0, 1)),
        "mskT": mskT,
        "bsT": np.ascontiguousarray(np.repeat(f(sgu_b_s)[0].reshape(4, 2, 1, 128), 64, axis=2)
                                    .reshape(4, 128, 128).transpose(1, 0, 2)),
        "ident": ident, "tm": tm, "ngm": ngm,
    }
    in_maps = []
    for c in range(8):
        b, half = c // 2, c % 2
        colsv = np.zeros((128, NCOL), np.float32)
        colsv[:, 0:8] = _col(ffn1_pre_g, 8)
        colsv[:, 8:16] = _col(ffn1_post_g, 8)
        colsv[:, 16:24] = _col(mix_pre_g, 8)
        colsv[:, 24:32] = _col(mix_post_g, 8)
        colsv[:, 32:40] = _col(ffn2_pre_g, 8)
        colsv[:, 40:48] = _col(ffn2_post_g, 8)
        colsv[:, 48:52] = _col(g_out_sb, 4)
        colsv[:, 52:56] = _col(g_out_sgu, 4)
        colsv[:, 56] = float(half)
        m = dict(shared)
        m["x_in"] = np.concatenate([x_prompt[b, half * NP:(half + 1) * NP], x_sample[c]], axis=0)
        m["ck"] = ck[0, c].reshape(PAST, 512)
        m["cv"] = cv[0, c].reshape(PAST, 512)
        m["cols"] = colsv
        in_maps.append(m)
    if "nc" not in _NC_CACHE:
        _NC_CACHE["nc"] = build_program()
    res = run_bass_kernel_spmd(_NC_CACHE["nc"], in_maps, core_ids=list(range(8)))
    R = res.results
    y_p = np.zeros((4, 4096, D), np.float32)
    y_s = np.zeros((8, NS, D), np.float32)
    k_p = np.zeros((1, 4, 4096, 8, 64), np.float32)
    v_p = np.zeros((1, 4, 4096, 8, 64), np.float32)
    k_s = np.zeros((1, 8, NS, 8, 64), np.float32)
    v_s = np.zeros((1, 8, NS, 8, 64), np.float32)
    g_s = np.zeros((1, 8, NS, 8, 64), np.float32)
    for c in range(8):
        b, half = c // 2, c % 2
        y = np.asarray(R[c]["y"])
        ko = np.asarray(R[c]["k_o"])
        vo = np.asarray(R[c]["v_o"])
        go = np.asarray(R[c]["g_o"])
        y_p[b, half * NP:(half + 1) * NP] = y[0:NP]
        y_s[c] = y[NP:NT]
        k_p[0, b, half * NP:(half + 1) * NP] = ko[0:NP].reshape(NP, 8, 64)
        v_p[0, b, half * NP:(half + 1) * NP] = vo[0:NP].reshape(NP, 8, 64)
        k_s[0, c] = ko[NP:NT].reshape(NS, 8, 64)
        v_s[0, c] = vo[NP:NT].reshape(NS, 8, 64)
        g_s[0, c] = go.reshape(NS, 8, 64)
    return (y_p, y_s, k_p, v_p, k_s, v_s, g_s)
```

```python
import numpy as np
import ml_dtypes
from contextlib import ExitStack
import concourse.bass as bass
import concourse.mybir as mybir
from concourse.bass_utils import run_bass_kernel_spmd

F32 = mybir.dt.float32
BF16 = mybir.dt.bfloat16
AF = mybir.ActivationFunctionType
ALU = mybir.AluOpType
AX = mybir.AxisListType

D = 1024
DC = 8
FF = 2816
FC = 22
NP = 2048
NS = 64
NT = NP + NS
PAST = 2048
EPS = 1e-6
NEG = -30000.0
SGS = [(0, 512), (512, 512), (1024, 512), (1536, 512), (2048, 64)]
GROUPS = [(0, SGS[0:2]), (1024, SGS[2:5])]
GW = 1088
NCOL = 64

STOP_AFTER = None
MIX_STOP = 99
SKIP_FFN1 = False
NO_CC = False
OVERLAP_CC = True


class Buf:
    __slots__ = ("w", "r")

    def __init__(self):
        self.w = None
        self.r = {}


class Prog:
    ENG = ("pe", "act", "dve", "pool", "sp")

    def __init__(self):
        self.st = {e: [] for e in self.ENG}
        self.cnt = {}
        self.base = {}

    def _deps(self, reads, writes, extra):
        d = dict(self.base)

        def add(k, v):
            if d.get(k, 0) < v:
                d[k] = v
        for b in reads:
            if b.w is not None:
                add(*b.w)
        for b in writes:
            if b.w is not None:
                add(*b.w)
            for k, v in b.r.items():
                add(k, v)
        for t in extra:
            if t is not None:
                add(*t)
        return d

    def _post(self, tok, reads, writes):
        k, v = tok
        for b in reads:
            if b.r.get(k, 0) < v:
                b.r[k] = v
        for b in writes:
            b.w = tok
            b.r = {}

    def op(self, eng, fn, reads=(), writes=(), extra=(), sem=None, inc=1):
        d = self._deps(reads, writes, extra)
        sem = sem or eng
        self.cnt[sem] = self.cnt.get(sem, 0) + inc
        tok = (sem, self.cnt[sem])
        self.st[eng].append((fn, d, sem, inc))
        self._post(tok, reads, writes)
        return tok

    def dma(self, fn, sem, reads=(), writes=(), extra=(), eng="sp"):
        return self.op(eng, fn, reads, writes, extra, sem="d_" + sem, inc=16)

    def barrier(self):
        self.base = dict(self.cnt)

    def emit(self, nc, block, sems):
        def run(name):
            def body(e):
                seen = {}
                for fn, d, sem, inc in self.st[name]:
                    for k, v in d.items():
                        if seen.get(k, 0) >= v:
                            continue
                        e.wait_ge(sems[k], v)
                        seen[k] = v
                    fn(e).then_inc(sems[sem], inc)
                if name == "sp":
                    for k, v in self.cnt.items():
                        if seen.get(k, 0) < v:
                            e.wait_ge(sems[k], v)
            return body
        block.tensor(run("pe"))
        block.scalar(run("act"))
        block.vector(run("dve"))
        block.gpsimd(run("pool"))
        block.sync(run("sp"))


class Arena:
    def __init__(self, ap32):
        self.ap = ap32
        self.off = 0
        self.cap = ap32.shape[1]

    def reset(self, off=0):
        self.off = off

    def f32(self, *shape):
        fr = shape[1:]
        n = int(np.prod(fr))
        v = self.ap[:, self.off:self.off + n]
        self.off += n
        assert self.off <= self.cap, (self.off, self.cap)
        return self._shape(v, fr)

    def bf16(self, *shape):
        fr = shape[1:]
        n = int(np.prod(fr))
        w = (n + 1) // 2
        v = self.ap[:, self.off:self.off + w].bitcast(BF16)[:, 0:n]
        self.off += w
        assert self.off <= self.cap, (self.off, self.cap)
        return self._shape(v, fr)

    @staticmethod
    def _shape(v, fr):
        if len(fr) == 1:
            return v
        if len(fr) == 2:
            return v.rearrange("p (a b) -> p a b", b=fr[1])
        return v.rearrange("p (a b c) -> p a b c", b=fr[1], c=fr[2])


def build_program():
    nc = bass.Bass("TRN2", target_bir_lowering=False)
    P = Prog()

    def din(name, shape, dt=F32):
        return nc.dram_tensor(name, list(shape), dt, kind="ExternalInput").ap()

    def dout(name, shape, dt=F32):
        return nc.dram_tensor(name, list(shape), dt, kind="ExternalOutput").ap()

    x_in = din("x_in", [NT, D])
    ck_in = din("ck", [PAST, 512])
    cv_in = din("cv", [PAST, 512])
    wts = {}
    for f in ("ffn1", "ffn2"):
        wts[f] = (din(f + "_wg", [D, FF]), din(f + "_wu", [D, FF]), din(f + "_wd", [FF, D]))
    w_in = din("w_in", [D, 2560])
    w_out = din("w_out", [D, D])
    cols_in = din("cols", [128, NCOL])
    lng_in = din("lng_b", [128, 512])
    lnb_in = din("lnb_b", [128, 512])
    wsT_in = din("wsT", [128, 8, 128])
    mskT_in = din("mskT", [128, 128])
    bsT_in = din("bsT", [128, 4, 128])
    ident_in = din("ident", [128, 128])
    tm_in = din("tm", [128, 128])
    ngm_in = din("ngm", [128, 128])

    y_out = dout("y", [NT, D])
    k_out = dout("k_o", [NT, 512])
    v_out = dout("v_o", [NT, 512])
    g_out = dout("g_o", [NS, 512])

    sndKt = nc.dram_tensor("sndK", [512, 2048], BF16)
    sndVt = nc.dram_tensor("sndV", [512, 2048], BF16)
    rcvKt = nc.dram_tensor("rcvK", [1024, 2048], BF16)
    rcvVt = nc.dram_tensor("rcvV", [1024, 2048], BF16)
    sndK = sndKt.ap().rearrange("(h p) t -> p h t", p=128)
    sndV = sndVt.ap().rearrange("r (a c) -> (r a) c", c=512)
    rcvK = rcvKt.ap()[0:512, :].rearrange("(h p) t -> p h t", p=128)
    rcvV = rcvVt.ap()[0:512, :].rearrange("r (a c) -> (r a) c", c=512)

    es = ExitStack()
    with es:
        def sb(name, shape, dt):
            return es.enter_context(nc.sbuf_tensor("sb_" + name, list(shape), dt))

        xT = sb("xT", [128, DC, NT], F32)
        cols = sb("cols", [128, NCOL], F32)
        identf = sb("identf", [128, 128], F32)
        identb = sb("identb", [128, 128], BF16)
        onesb = sb("onesb", [128, 128], BF16)
        negonesb = sb("negonesb", [128, 128], BF16)
        tmb = sb("tmb", [128, 128], BF16)
        ngmb = sb("ngmb", [128, 128], BF16)
        wsTm = sb("wsTm", [128, 8, 128], BF16)
        bsT = sb("bsT", [128, 4, 128], F32)
        lngb = sb("lngb", [128, 512], F32)
        lnbb = sb("lnbb", [128, 512], F32)
        onecol = sb("onecol", [128, 1], F32)
        epscol = sb("epscol", [128, 1], F32)
        KTs = sb("KTs", [128, 4, 128], BF16)
        Vs = sb("Vs", [128, 512], BF16)
        AW = 32256
        arena_t = sb("arena", [128, AW], F32)
        A = Arena(arena_t[:, :])
        psum = [es.enter_context(nc.psum_tensor(f"ps{i}", [128, 512], F32)) for i in range(8)]
        psb = [Buf() for _ in range(8)]

        B_xT = [Buf() for _ in SGS]
        B_const = Buf()

        def sg_index(c0):
            return [i for i, (a, n) in enumerate(SGS) if a == c0][0]

        C_F1PRE, C_F1POST, C_MPRE, C_MPOST, C_F2PRE, C_F2POST, C_GSB, C_GSGU, C_FLAG = 0, 8, 16, 24, 32, 40, 48, 52, 56

        A.reset()
        t_f = A.f32(128, 8, 128)
        t_m = A.f32(128, 128)
        t_t = A.f32(128, 128)
        t_n = A.f32(128, 128)
        B_t = Buf()
        P.dma(lambda e: e.dma_start(out=cols[:], in_=cols_in), "c")
        P.dma(lambda e: e.dma_start(out=identf[:], in_=ident_in), "c")
        P.dma(lambda e: e.dma_start(out=bsT[:], in_=bsT_in), "c")
        P.dma(lambda e: e.dma_start(out=lngb[:], in_=lng_in), "c")
        P.dma(lambda e: e.dma_start(out=lnbb[:], in_=lnb_in), "c")
        P.dma(lambda e: e.dma_start(out=t_f, in_=wsT_in), "c")
        P.dma(lambda e: e.dma_start(out=t_m, in_=mskT_in), "c")
        P.dma(lambda e: e.dma_start(out=t_t, in_=tm_in), "c")
        P.dma(lambda e: e.dma_start(out=t_n, in_=ngm_in), "c")
        B_const.w = B_t.w = ("d_c", P.cnt["d_c"])
        P.op("pool", lambda e: e.memset(onesb[:], 1.0), writes=[B_const])
        P.op("pool", lambda e: e.memset(KTs[:], 0.0), writes=[B_const])
        P.op("pool", lambda e: e.memset(Vs[:], 0.0), writes=[B_const])
        P.op("pool", lambda e: e.memset(negonesb[:], -1.0), writes=[B_const])
        P.op("pool", lambda e: e.memset(onecol[:], 1.0), writes=[B_const])
        P.op("pool", lambda e: e.memset(epscol[:], EPS), writes=[B_const])
        P.op("dve", lambda e: e.tensor_copy(out=identb[:], in_=identf[:]), reads=[B_const], writes=[B_const])
        P.op("dve", lambda e: e.tensor_copy(out=tmb[:], in_=t_t), reads=[B_t], writes=[B_const])
        P.op("dve", lambda e: e.tensor_copy(out=ngmb[:], in_=t_n), reads=[B_t], writes=[B_const])
        for g in range(8):
            P.op("dve", lambda e, g=g: e.tensor_tensor(out=wsTm[:, g, :], in0=t_f[:, g, :], in1=t_m, op=ALU.mult),
                 reads=[B_t], writes=[B_const])
        for c in (C_F1POST, C_F2POST):
            P.op("dve", lambda e, c=c: e.tensor_scalar(out=cols[:, c:c + 8], in0=cols[:, c:c + 8], scalar1=0.5,
                                                       scalar2=None, op0=ALU.mult), reads=[B_const], writes=[B_const])
        P.barrier()

        A.reset()
        xst = [A.f32(128, D) for _ in range(4)]
        B_xst = [Buf() for _ in range(4)]
        tiles = [(t * 128, 128) for t in range(16)] + [(NP, NS)]
        for ti, (r0, rows) in enumerate(tiles):
            s = ti % 4
            P.dma(lambda e, s=s, r0=r0, rows=rows: e.dma_start(out=xst[s][0:rows, :], in_=x_in[r0:r0 + rows, :]),
                  "x%d" % s, writes=[B_xst[s]])
            for hb in range(2):
                bank = (2 * ti + hb) % 4

                def tr(e, s=s, hb=hb, bank=bank, rows=rows):
                    ins = None
                    for q in range(4):
                        dc = hb * 4 + q
                        ins = e.transpose(out=psum[bank][:, q * 128:q * 128 + rows],
                                          in_=xst[s][0:rows, dc * 128:(dc + 1) * 128],
                                          identity=identf[0:rows, 0:rows])
                    return ins
                P.op("pe", tr, reads=[B_xst[s], B_const], writes=[psb[bank]])
                sgi = min(r0 // 512, 4)
                eng = "dve" if hb == 0 else "act"

                def ev(e, hb=hb, bank=bank, rows=rows, r0=r0, eng=eng):
                    src = psum[bank][:, :].rearrange("p (q c) -> p q c", c=128)[:, :, 0:rows]
                    dst = xT[:, hb * 4:hb * 4 + 4, r0:r0 + rows]
                    if eng == "dve":
                        return e.tensor_copy(out=dst, in_=src)
                    return e.activation(out=dst, in_=src, func=AF.Copy)
                P.op(eng, ev, reads=[psb[bank]], writes=[B_xT[sgi]])
        P.barrier()

        def rms_rstd(src_sq_fn, nchunks, n, ps_bank, rstd_ap, B_rstd, reads, sq_ap, B_sq, scale):
            src_sq_fn()
            P.op("pe", lambda e: _mm_chain(e, psum[ps_bank][:, 0:n], [(onesb[:, :], sq_ap[:, c, 0:n]) for c in range(nchunks)]),
                 reads=[B_sq, B_const], writes=[psb[ps_bank]])
            P.op("act", lambda e: e.activation(out=rstd_ap, in_=psum[ps_bank][:, 0:n], func=AF.Ln, bias=epscol[:], scale=scale),
                 reads=[psb[ps_bank], B_const], writes=[B_rstd])
            P.op("act", lambda e: e.activation(out=rstd_ap, in_=rstd_ap, func=AF.Exp, scale=-0.5),
                 reads=[B_rstd], writes=[B_rstd])

        def _mm_chain(e, out, pairs, tp=None, first=True, last=True):
            ins = None
            n = len(pairs)
            for i, pr in enumerate(pairs):
                l, r = pr[0], pr[1]
                kw = {}
                if len(pr) > 2:
                    kw["tile_position"] = pr[2]
                elif tp is not None:
                    kw["tile_position"] = tp
                ins = e.matmul(out, lhsT=l, rhs=r, start=(first and i == 0), stop=(last and i == n - 1), **kw)
            return ins

        def prenorm(gstart, sgs, gcol, hT, B_hT, sq, B_sq, rstd, B_rstd):
            for si, (c0, n) in enumerate(sgs):
                lc = c0 - gstart
                bx = B_xT[sg_index(c0)]
                s2 = si % 2

                def sqf(c0=c0, n=n, s2=s2, bx=bx):
                    P.op("pool", lambda e: e.tensor_tensor(out=sq[s2][:, :, 0:n], in0=xT[:, :, c0:c0 + n],
                                                           in1=xT[:, :, c0:c0 + n], op=ALU.mult),
                         reads=[bx], writes=[B_sq[s2]])
                rms_rstd(sqf, DC, n, 7, rstd[s2][:, 0:n], B_rstd[s2], [bx], sq[s2], B_sq[s2], 1.0 / D)
                for dc in range(DC):
                    P.op("dve", lambda e, dc=dc, c0=c0, n=n, lc=lc, s2=s2: e.scalar_tensor_tensor(
                        out=hT[:, dc, lc:lc + n], in0=xT[:, dc, c0:c0 + n], scalar=cols[:, gcol + dc:gcol + dc + 1],
                        in1=rstd[s2][:, 0:n], op0=ALU.mult, op1=ALU.mult),
                        reads=[bx, B_rstd[s2], B_const], writes=[B_hT[si]])

        def postnorm_residual(gstart, sgs, gcol, ysb, B_ysb, sq, B_sq, rstd, B_rstd):
            for si, (c0, n) in enumerate(sgs):
                lc = c0 - gstart
                bx = B_xT[sg_index(c0)]
                s2 = si % 2

                def sqf(lc=lc, n=n, s2=s2, si=si):
                    P.op("pool", lambda e: e.tensor_tensor(out=sq[s2][:, :, 0:n], in0=ysb[:, :, lc:lc + n],
                                                           in1=ysb[:, :, lc:lc + n], op=ALU.mult),
                         reads=[B_ysb[si]], writes=[B_sq[s2]])
                rms_rstd(sqf, DC, n, 7, rstd[s2][:, 0:n], B_rstd[s2], [B_ysb[si]], sq[s2], B_sq[s2], 1.0 / D)
                for dc in range(DC):
                    eng = "dve" if dc % 2 == 0 else "pool"
                    P.op(eng, lambda e, dc=dc, lc=lc, n=n, s2=s2: e.tensor_tensor(
                        out=ysb[:, dc, lc:lc + n], in0=ysb[:, dc, lc:lc + n], in1=rstd[s2][:, 0:n], op=ALU.mult),
                        reads=[B_rstd[s2]], writes=[B_ysb[si]])
                    P.op("dve", lambda e, dc=dc, c0=c0, n=n, lc=lc: e.scalar_tensor_tensor(
                        out=xT[:, dc, c0:c0 + n], in0=ysb[:, dc, lc:lc + n], scalar=cols[:, gcol + dc:gcol + dc + 1],
                        in1=xT[:, dc, c0:c0 + n], op0=ALU.mult, op1=ALU.add),
                        reads=[B_ysb[si], B_const], writes=[bx])

        def ffn(name, c_pre, c_post):
            wg, wu, wd = wts[name]
            wgv = wg.rearrange("(kc p) f -> p kc f", p=128)
            wuv = wu.rearrange("(kc p) f -> p kc f", p=128)
            wdv = wd.rearrange("(fc p) m -> p fc m", p=128)
            for gstart, sgs in GROUPS:
                A.reset()
                off0 = A.off
                hT = A.bf16(128, DC, GW)
                gst32 = [[A.f32(128, DC, 128) for _ in range(2)] for _ in range(2)]
                gstb = [[A.bf16(128, DC, 128) for _ in range(2)] for _ in range(2)]
                end_alias = A.off
                A.reset(off0)
                ysb = A.f32(128, DC, GW)
                assert A.off <= end_alias
                A.reset(end_alias)
                hidT = A.bf16(128, FC, GW)
                dst32 = [A.f32(128, 11, 128) for _ in range(2)]
                dstb = [A.bf16(128, FC, 128) for _ in range(2)]
                sq1 = A.bf16(128, DC, 512)
                sq = [sq1, sq1]
                rstd = [A.f32(128, 512) for _ in range(2)]
                sgt = [A.f32(128, 512) for _ in range(2)]
                B_hT = [Buf() for _ in sgs]
                B_hid = [Buf() for _ in sgs]
                B_ysb = [Buf() for _ in sgs]
                B_sq1 = Buf()
                B_sq = [B_sq1, B_sq1]
                B_rstd = [Buf(), Buf()]
                B_g32 = [[Buf(), Buf()], [Buf(), Buf()]]
                B_gb = [[Buf(), Buf()], [Buf(), Buf()]]
                B_d32 = [Buf(), Buf()]
                B_db = [[Buf(), Buf()], [Buf(), Buf()]]
                B_sgt = [Buf(), Buf()]

                prenorm(gstart, sgs, c_pre, hT, B_hT, sq, B_sq, rstd, B_rstd)

                k = 0
                for fb in range(FC):
                    s = fb % 2
                    P.dma(lambda e, s=s, fb=fb: e.dma_start(out=gst32[s][0], in_=wgv[:, :, fb * 128:(fb + 1) * 128]),
                          "g%d" % s, writes=[B_g32[s][0]])
                    P.dma(lambda e, s=s, fb=fb: e.dma_start(out=gst32[s][1], in_=wuv[:, :, fb * 128:(fb + 1) * 128]),
                          "u%d" % s, writes=[B_g32[s][1]])
                    P.op("pool", lambda e, s=s: e.tensor_copy(out=gstb[s][0], in_=gst32[s][0]), reads=[B_g32[s][0]], writes=[B_gb[s][0]])
                    P.op("act", lambda e, s=s: e.activation(out=gstb[s][1], in_=gst32[s][1], func=AF.Copy), reads=[B_g32[s][1]], writes=[B_gb[s][1]])
                    for si, (c0, n) in enumerate(sgs):
                        lc = c0 - gstart
                        bg, bu = (2 * k) % 6, (2 * k + 1) % 6
                        k += 1
                        P.op("pe", lambda e, s=s, lc=lc, n=n, bg=bg: _mm_chain(
                            e, psum[bg][:, 0:n], [(gstb[s][0][:, kc, :], hT[:, kc, lc:lc + n]) for kc in range(DC)]),
                            reads=[B_gb[s][0], B_hT[si]], writes=[psb[bg]])
                        P.op("pe", lambda e, s=s, lc=lc, n=n, bu=bu: _mm_chain(
                            e, psum[bu][:, 0:n], [(gstb[s][1][:, kc, :], hT[:, kc, lc:lc + n]) for kc in range(DC)]),
                            reads=[B_gb[s][1], B_hT[si]], writes=[psb[bu]])
                        s2 = k % 2
                        P.op("act", lambda e, n=n, bg=bg, s2=s2: e.activation(out=sgt[s2][:, 0:n], in_=psum[bg][:, 0:n], func=AF.Silu),
                             reads=[psb[bg]], writes=[B_sgt[s2]])
                        P.op("dve", lambda e, n=n, bu=bu, s2=s2, fb=fb, lc=lc: e.tensor_tensor(
                            out=hidT[:, fb, lc:lc + n], in0=sgt[s2][:, 0:n], in1=psum[bu][:, 0:n], op=ALU.mult),
                            reads=[B_sgt[s2], psb[bu]], writes=[B_hid[si]])
                P.barrier()

                k = 0
                for dmc in range(DC):
                    s = dmc % 2
                    for hh in range(2):
                        P.dma(lambda e, s=s, dmc=dmc, hh=hh: e.dma_start(
                            out=dst32[hh], in_=wdv[:, hh * 11:(hh + 1) * 11, dmc * 128:(dmc + 1) * 128]),
                            "d%d" % hh, writes=[B_d32[hh]])
                        if hh == 0:
                            P.op("dve", lambda e, s=s, hh=hh: e.tensor_copy(out=dstb[s][:, hh * 11:(hh + 1) * 11, :], in_=dst32[hh]),
                                 reads=[B_d32[hh]], writes=[B_db[s][hh]])
                        else:
                            P.op("act", lambda e, s=s, hh=hh: e.activation(out=dstb[s][:, hh * 11:(hh + 1) * 11, :], in_=dst32[hh], func=AF.Copy),
                                 reads=[B_d32[hh]], writes=[B_db[s][hh]])
                    for si, (c0, n) in enumerate(sgs):
                        lc = c0 - gstart
                        bank = k % 6
                        k += 1
                        P.op("pe", lambda e, s=s, lc=lc, n=n, bank=bank: _mm_chain(
                            e, psum[bank][:, 0:n], [(dstb[s][:, fc, :], hidT[:, fc, lc:lc + n]) for fc in range(FC)]),
                            reads=[B_db[s][0], B_db[s][1], B_hid[si]], writes=[psb[bank]])
                        P.op("act", lambda e, n=n, bank=bank, dmc=dmc, lc=lc: e.activation(
                            out=ysb[:, dmc, lc:lc + n], in_=psum[bank][:, 0:n], func=AF.Copy),
                            reads=[psb[bank]], writes=[B_ysb[si]])
                postnorm_residual(gstart, sgs, c_post, ysb, B_ysb, sq, B_sq, rstd, B_rstd)
                P.barrier()

        def mix_phase():
            w_inv = w_in.rearrange("(kc p) f -> p kc f", p=128)
            w_outv = w_out.rearrange("(fc p) m -> p fc m", p=128)
            A.reset()
            QT = A.bf16(128, 4, NT)
            QT_WORDS = A.off
            mT = A.bf16(128, 8, NT)
            ssb = A.f32(128, NT)
            keep = A.off
            B_QT, B_mT, B_ssb = Buf(), Buf(), Buf()
            B_snd, B_rcv = Buf(), Buf()
            hTm = A.bf16(128, DC, 512)
            wstb = [A.bf16(128, DC, 256) for _ in range(2)]
            uT = A.f32(128, 4, 512)
            gG = A.f32(128, 4, 512)
            kvtm = A.f32(128, 4, 512)
            kvb = A.bf16(128, 4, 512)
            gnb = A.bf16(128, 4, 512)
            sq1 = A.bf16(128, DC, 512)
            rstd1 = A.f32(128, 512)
            stat = A.f32(128, 16)
            p1_end = A.off
            A.reset(QT_WORDS)
            wst32 = [A.f32(128, DC, 256) for _ in range(2)]
            assert A.off <= QT_WORDS + 4224
            A.reset(QT_WORDS + 8448)
            osgG = A.f32(128, 4, 512)
            assert A.off <= keep
            A.reset(p1_end)
            B_hTm, B_w32, B_wb = [Buf()], [Buf(), Buf()], [Buf(), Buf()]
            B_uT, B_gG, B_kvtm, B_kvb, B_gnb, B_sq, B_rstd, B_stat, B_osg = Buf(), Buf(), Buf(), Buf(), Buf(), Buf(), Buf(), Buf(), Buf()
            pk = [0]
            wk = [0]

            def bank6():
                pk[0] += 1
                return pk[0] % 6

            for gi, (c0, n) in enumerate(SGS):
                samp = (c0 == NP)
                gt = [(0, 64)] if samp else [(t * 128, 128) for t in range(4)]
                nt_ = len(gt)
                rows = gt[0][1]
                prenorm(c0, [(c0, n)], C_MPRE, hTm, B_hTm, [sq1, sq1], [B_sq, B_sq], [rstd1, rstd1], [B_rstd, B_rstd])

                def do_block(blk, c0=c0, n=n, samp=samp, gt=gt):
                    s = wk[0] % 2
                    wk[0] += 1
                    P.dma(lambda e: e.dma_start(out=wst32[s], in_=w_inv[:, :, blk * 256:(blk + 1) * 256]), "wi%d" % s, writes=[B_w32[s]])
                    P.op("act", lambda e: e.activation(out=wstb[s], in_=wst32[s], func=AF.Copy), reads=[B_w32[s]], writes=[B_wb[s]])
                    kind = blk // 2
                    half = blk % 2
                    if kind in (0, 1, 3):
                        for j in range(2):
                            hp = half * 2 + j
                            bk = bank6()
                            P.op("pe", lambda e, j=j, bk=bk: _mm_chain(
                                e, psum[bk][:, 0:n], [(wstb[s][:, kc, j * 128:(j + 1) * 128], hTm[:, kc, 0:n]) for kc in range(DC)]),
                                reads=[B_wb[s], B_hTm[0]], writes=[psb[bk]])
                            if kind == 0:
                                P.op("dve", lambda e, hp=hp, bk=bk: e.tensor_scalar(
                                    out=QT[:, hp, c0:c0 + n], in0=psum[bk][:, 0:n], scalar1=0.125, scalar2=None, op0=ALU.mult),
                                    reads=[psb[bk]], writes=[B_QT])
                            elif kind == 1:
                                dst = KTs[:, hp, 0:NS] if samp else kvb[:, hp, 0:n]
                                P.op("dve", lambda e, dst=dst, bk=bk: e.tensor_copy(out=dst, in_=psum[bk][:, 0:n]),
                                     reads=[psb[bk]], writes=[B_const if samp else B_kvb])
                            else:
                                P.op("act", lambda e, hp=hp, bk=bk: e.activation(out=uT[:, hp, 0:n], in_=psum[bk][:, 0:n], func=AF.Gelu),
                                     reads=[psb[bk]], writes=[B_uT])
                        if kind == 1 and half == 1 and not samp:
                            P.dma(lambda e: e.dma_start(out=sndK[:, :, c0:c0 + n], in_=kvb[:, :, 0:n]), "sk",
                                  reads=[B_kvb], writes=[B_snd])
                    if kind in (1, 2, 4):
                        for ti, (r0, rows_) in enumerate(gt):
                            bk = bank6()
                            P.op("pe", lambda e, bk=bk, r0=r0, rows_=rows_: _mm_chain(
                                e, psum[bk][0:rows_, 0:256], [(hTm[:, kc, r0:r0 + rows_], wstb[s][:, kc, :]) for kc in range(DC)]),
                                reads=[B_wb[s], B_hTm[0]], writes=[psb[bk]])
                            cs = slice(half * 256, half * 256 + 256)
                            if kind == 4:
                                P.op("act", lambda e, bk=bk, ti=ti, rows_=rows_, cs=cs: e.activation(
                                    out=gG[0:rows_, ti, cs], in_=psum[bk][0:rows_, 0:256], func=AF.Gelu),
                                    reads=[psb[bk]], writes=[B_gG])
                            else:
                                P.op("dve", lambda e, bk=bk, ti=ti, rows_=rows_, cs=cs: e.tensor_copy(
                                    out=kvtm[0:rows_, ti, cs], in_=psum[bk][0:rows_, 0:256]),
                                    reads=[psb[bk]], writes=[B_kvtm])
                        if kind in (1, 2) and half == 1:
                            dsto = k_out if kind == 1 else v_out
                            if samp:
                                P.dma(lambda e: e.dma_start(out=dsto[NP:NT, :], in_=kvtm[0:NS, 0, :]), "okv", reads=[B_kvtm])
                            else:
                                P.dma(lambda e: e.dma_start(
                                    out=dsto[c0:c0 + 512, :].rearrange("(t p) c -> p t c", p=128), in_=kvtm[:, :, :]), "okv", reads=[B_kvtm])
                            if kind == 2:
                                if samp:
                                    P.op("pool", lambda e: e.tensor_copy(out=Vs[0:NS, :], in_=kvtm[0:NS, 0, :]), reads=[B_kvtm], writes=[B_const])
                                else:
                                    P.op("pool", lambda e: e.tensor_copy(out=kvb[:, :, :], in_=kvtm[:, :, :]), reads=[B_kvtm], writes=[B_kvb])
                                    P.dma(lambda e: e.dma_start(
                                        out=sndV[c0:c0 + 512, :].rearrange("(t p) c -> p t c", p=128), in_=kvb[:, :, :]), "sv",
                                        reads=[B_kvb], writes=[B_snd])

                def tail1(rows=rows, nt_=nt_, samp=samp):
                    for ti in range(nt_):
                        P.op("dve", lambda e, ti=ti: e.reduce_sum(out=stat[0:rows, ti:ti + 1], in_=gG[0:rows, ti, :], axis=AX.X),
                             reads=[B_gG], writes=[B_stat])
                    P.op("dve", lambda e: e.tensor_scalar(out=stat[0:rows, 0:nt_], in0=stat[0:rows, 0:nt_],
                                                          scalar1=1.0 / 512, scalar2=None, op0=ALU.mult),
                         reads=[B_stat], writes=[B_stat])
                    for ti in range(nt_):
                        P.op("dve", lambda e, ti=ti: e.tensor_scalar(out=gG[0:rows, ti, :], in0=gG[0:rows, ti, :],
                                                                     scalar1=stat[0:rows, ti:ti + 1], scalar2=None, op0=ALU.subtract),
                             reads=[B_stat], writes=[B_gG])
                        P.op("pool", lambda e, ti=ti: e.tensor_tensor(out=osgG[0:rows, ti, :], in0=gG[0:rows, ti, :],
                                                                      in1=gG[0:rows, ti, :], op=ALU.mult),
                             reads=[B_gG], writes=[B_osg])
                        P.op("dve", lambda e, ti=ti: e.reduce_sum(out=stat[0:rows, 4 + ti:5 + ti], in_=osgG[0:rows, ti, :], axis=AX.X),
                             reads=[B_osg], writes=[B_stat])
                    P.op("act", lambda e: e.activation(out=stat[0:rows, 8:8 + nt_], in_=stat[0:rows, 4:4 + nt_], func=AF.Ln,
                                                       bias=epscol[0:rows, :], scale=1.0 / 512),
                         reads=[B_stat, B_const], writes=[B_stat])
                    P.op("act", lambda e: e.activation(out=stat[0:rows, 8:8 + nt_], in_=stat[0:rows, 8:8 + nt_], func=AF.Exp, scale=-0.5),
                         reads=[B_stat], writes=[B_stat])
                    for ti in range(nt_):
                        P.op("dve", lambda e, ti=ti: e.scalar_tensor_tensor(
                            out=gG[0:rows, ti, :], in0=gG[0:rows, ti, :], scalar=stat[0:rows, 8 + ti:9 + ti], in1=lngb[0:rows, :],
                            op0=ALU.mult, op1=ALU.mult), reads=[B_stat, B_const], writes=[B_gG])
                        P.op("pool", lambda e, ti=ti: e.tensor_tensor(out=gG[0:rows, ti, :], in0=gG[0:rows, ti, :],
                                                                      in1=lnbb[0:rows, :], op=ALU.add),
                             reads=[B_const], writes=[B_gG])
                        P.op("pool", lambda e, ti=ti: e.tensor_copy(out=gnb[0:rows, ti, :], in_=gG[0:rows, ti, :]),
                             reads=[B_gG], writes=[B_gnb])
                    if samp:
                        P.dma(lambda e: e.dma_start(out=g_out[:, :], in_=gG[0:NS, 0, :]), "og", reads=[B_gG])

                def tail2(rows=rows, nt_=nt_, n=n, c0=c0):
                    for ti in range(nt_):
                        bk = bank6()

                        def sgu_mm(e, ti=ti, bk=bk):
                            ins = None
                            for gp in range(4):
                                for gl in range(2):
                                    g = 2 * gp + gl
                                    ins = e.matmul(psum[bk][gl * 64:(gl + 1) * 64, gp * 128:gp * 128 + rows],
                                                   lhsT=gnb[0:rows, ti, g * 64:(g + 1) * 64], rhs=wsTm[0:rows, g, 0:rows],
                                                   start=True, stop=True, tile_position=(0, gl * 64))
                            return ins
                        P.op("pe", sgu_mm, reads=[B_gnb, B_const], writes=[psb[bk]])
                        r0 = ti * 128
                        P.op("dve", lambda e, bk=bk, r0=r0: e.tensor_tensor(
                            out=osgG[:, :, r0:r0 + rows], in0=psum[bk][:, :].rearrange("p (a b) -> p a b", b=128)[:, :, 0:rows],
                            in1=bsT[:, :, 0:rows], op=ALU.add), reads=[psb[bk], B_const], writes=[B_osg])
                        P.op("pool", lambda e, r0=r0: e.tensor_tensor(
                            out=osgG[:, :, r0:r0 + rows], in0=osgG[:, :, r0:r0 + rows], in1=uT[:, :, r0:r0 + rows], op=ALU.mult),
                            reads=[B_uT], writes=[B_osg])

                    def sqf():
                        P.op("pool", lambda e: e.tensor_tensor(out=sq1[:, 0:4, 0:n], in0=osgG[:, :, 0:n], in1=osgG[:, :, 0:n], op=ALU.mult),
                             reads=[B_osg], writes=[B_sq])
                    rms_rstd(sqf, 4, n, 7, rstd1[:, 0:n], B_rstd, [B_osg], sq1, B_sq, 1.0 / 512)
                    for gp in range(4):
                        P.op("dve", lambda e, gp=gp: e.scalar_tensor_tensor(
                            out=mT[:, 4 + gp, c0:c0 + n], in0=osgG[:, gp, 0:n], scalar=cols[:, C_GSGU + gp:C_GSGU + gp + 1],
                            in1=rstd1[:, 0:n], op0=ALU.mult, op1=ALU.mult), reads=[B_osg, B_rstd, B_const], writes=[B_mT])

                for blk in (0, 1, 6, 7, 8, 9):
                    do_block(blk)
                tail1()
                for blk in (2, 3, 4, 5):
                    do_block(blk)
                tail2()
            if MIX_STOP <= 1:
                P.barrier()
                return
            for st_, rt_ in (() if NO_CC else ((sndKt, rcvKt), (sndVt, rcvVt))):
                P.op("pool", lambda e, st_=st_, rt_=rt_: e.collective_compute(
                    "AllGather", ALU.bypass, replica_groups=[[0, 1], [2, 3], [4, 5], [6, 7]], ins=[st_.ap()], outs=[rt_.ap()]),
                    reads=[B_snd], writes=[B_rcv], sem="cc", inc=1)
            if NO_CC:
                for st_, rt_ in ((sndKt, rcvKt), (sndVt, rcvVt)):
                    P.dma(lambda e, st_=st_, rt_=rt_: e.dma_start(out=rt_.ap()[0:512, :], in_=st_.ap()), "dbgcc",
                          reads=[B_snd], writes=[B_rcv])
            P.barrier()

            if MIX_STOP <= 2:
                return
            A.reset(keep)
            kvreg = A.off
            KTc = A.bf16(128, 4, PAST)
            Vc = A.bf16(128, 16, 512)
            cst32 = [A.f32(128, 512) for _ in range(2)]
            cstb = [A.bf16(128, 512) for _ in range(2)]
            A.reset(kvreg)
            KThp = [A.bf16(128, 4096) for _ in range(2)]
            Vhp = [A.bf16(128, 32, 128) for _ in range(2)]
            A.reset(kvreg + 9728)
            e32 = [A.f32(128, 512) for _ in range(2)]
            Lp = [A.bf16(128, 512) for _ in range(3)]
            At = [A.bf16(128, 512) for _ in range(2)]
            Ls32 = [A.f32(128, 512) for _ in range(2)]
            LsB = [A.bf16(128, 512) for _ in range(3)]
            osq = A.bf16(128, 512)
            otmp = A.f32(128, 512)
            B_otmp = Buf()
            QTm = [A.bf16(128, 512) for _ in range(2)]
            QTs8 = A.bf16(128, 8, NS)
            B_QTm, B_QTs8 = [Buf(), Buf()], Buf()
            B_KTc, B_Vc, B_c32, B_cb = Buf(), Buf(), [Buf(), Buf()], [Buf(), Buf()]
            B_KThp, B_Vhp = [Buf(), Buf()], [Buf(), Buf()]
            B_e32, B_Lp, B_At, B_Ls32, B_LsB, B_osq = [Buf(), Buf()], [Buf(), Buf(), Buf()], [Buf(), Buf()], [Buf(), Buf()], [Buf(), Buf(), Buf()], Buf()
            ctr = {"n": 0, "l": 0, "o": 0, "q": 0}

            pendB, pendC = [], []

            def attn_step(zfn, Efn, avfn, ks, a0, first, last, carry_bufs, obuf, pre=None, post=None):
                n = ctr["n"]
                ctr["n"] += 1
                zb, Eb, ei, li, ai = n % 2, 2 + n % 2, n % 2, n % 3, n % 2
                cur = ctr["l"] % 2
                nxt = 1 - cur
                if pre is not None:
                    pre()
                if first:
                    P.op("pool", lambda e: e.memset(Ls32[cur][:, :], 0.0), writes=[B_Ls32[cur]])
                P.op("pe", lambda e: zfn(e, psum[zb]), reads=[B_QT, B_const] + carry_bufs, writes=[psb[zb]])
                P.op("act", lambda e: e.activation(out=e32[ei][0:ks, a0:512], in_=psum[zb][0:ks, a0:512], func=AF.Exp),
                     reads=[psb[zb]], writes=[B_e32[ei]])
                if a0 > 0:
                    P.op("pool", lambda e: e.memset(Lp[li][:, 0:a0], 0.0), writes=[B_Lp[li]])
                P.op("act", lambda e: e.activation(out=Lp[li][0:ks, a0:512], in_=e32[ei][0:ks, a0:512], func=AF.Ln, bias=onecol[0:ks, :]),
                     reads=[B_e32[ei], B_const], writes=[B_Lp[li]])
                if not last:
                    P.op("dve", lambda e: e.tensor_tensor(out=Ls32[nxt][:, :], in0=Ls32[cur][:, :], in1=Lp[li][:, :], op=ALU.add),
                         reads=[B_Ls32[cur], B_Lp[li]], writes=[B_Ls32[nxt]])
                    P.op("dve", lambda e: e.tensor_tensor(out=LsB[(n + 1) % 3][:, :], in0=Ls32[cur][:, :], in1=Lp[li][:, :], op=ALU.add),
                         reads=[B_Ls32[cur], B_Lp[li]], writes=[B_LsB[(n + 1) % 3]])
                    ctr["l"] += 1

                def partB():
                    rd = [B_QT, B_const, B_Lp[li]] + carry_bufs + ([] if first else [B_LsB[n % 3]])
                    P.op("pe", lambda e: Efn(e, psum[Eb], Lp[li], None if first else LsB[n % 3]), reads=rd, writes=[psb[Eb]])
                    if a0 > 0:
                        P.op("pool", lambda e: e.memset(At[ai][:, 0:a0], 0.0), writes=[B_At[ai]])
                    P.op("act", lambda e: e.activation(out=At[ai][0:ks, a0:512], in_=psum[Eb][0:ks, a0:512], func=AF.Exp),
                         reads=[psb[Eb]], writes=[B_At[ai]])

                def partC():
                    P.op("pe", lambda e: avfn(e, At[ai], first, last), reads=[B_At[ai]] + carry_bufs, writes=[obuf])
                    if post is not None:
                        post()
                if pendB:
                    pb, pc = pendB.pop(0)
                    pb()
                    if pendC:
                        pendC.pop(0)()
                    pendC.append(pc)
                pendB.append((partB, partC))

            def attn_flush():
                while pendB:
                    pb, pc = pendB.pop(0)
                    pb()
                    if pendC:
                        pendC.pop(0)()
                    pendC.append(pc)
                while pendC:
                    pendC.pop(0)()

            if OVERLAP_CC:
                P.base.pop("cc", None)
            for blk in range(16):
                s = blk % 2
                P.dma(lambda e, s=s, blk=blk: e.dma_start(out=cst32[s], in_=ck_in[blk * 128:(blk + 1) * 128, :]), "ck%d" % s, writes=[B_c32[s]])
                P.op("act", lambda e, s=s: e.activation(out=cstb[s], in_=cst32[s], func=AF.Copy), reads=[B_c32[s]], writes=[B_cb[s]])
                bk = 6 + blk % 2

                def trc(e, bk=bk, s=s):
                    ins = None
                    pv = psum[bk][:, :].bitcast(BF16)
                    for hp in range(4):
                        ins = e.transpose(out=pv[:, hp * 128:(hp + 1) * 128], in_=cstb[s][:, hp * 128:(hp + 1) * 128], identity=identb[:, :])
                    return ins
                P.op("pe", trc, reads=[B_cb[s], B_const], writes=[psb[bk]])
                P.op("dve", lambda e, bk=bk, blk=blk: e.tensor_copy(
                    out=KTc[:, :, blk * 128:(blk + 1) * 128],
                    in_=psum[bk][:, :].bitcast(BF16)[:, 0:512].rearrange("p (a b) -> p a b", b=128)),
                    reads=[psb[bk]], writes=[B_KTc])
                s2 = (blk + 1) % 2
                P.dma(lambda e, s2=s2, blk=blk: e.dma_start(out=cst32[s2], in_=cv_in[blk * 128:(blk + 1) * 128, :]), "ck%d" % s2, writes=[B_c32[s2]])
                P.op("act", lambda e, s2=s2, blk=blk: e.activation(out=Vc[:, blk, :], in_=cst32[s2], func=AF.Copy), reads=[B_c32[s2]], writes=[B_Vc])

            QS = slice(NP, NT)
            P.op("pool", lambda e: e.memset(QTs8[:, :, :], 0.0), writes=[B_QTs8])
            for h in range(8):
                P.op("pool", lambda e, h=h: e.tensor_copy(out=QTs8[(h % 2) * 64:(h % 2 + 1) * 64, h, :],
                                                          in_=QT[(h % 2) * 64:(h % 2 + 1) * 64, h // 2, NP:NT]),
                     reads=[B_QT], writes=[B_QTs8])
            def sample_epilogue():
                for hp in range(4):
                    P.op("dve", lambda e, hp=hp: e.tensor_scalar(out=mT[:, hp, QS], in0=psum[4][:, hp * 128:hp * 128 + 64],
                                                                 scalar1=cols[:, C_GSB + hp:C_GSB + hp + 1], scalar2=None, op0=ALU.mult),
                         reads=[psb[4], B_const], writes=[B_mT])
                for hp in range(4):
                    P.op("dve", lambda e, hp=hp: e.tensor_copy(out=otmp[:, hp * 64:(hp + 1) * 64], in_=psum[4][:, hp * 128:hp * 128 + 64]),
                         reads=[psb[4]], writes=[B_otmp])
                P.op("pool", lambda e: e.tensor_tensor(out=osq[:, 0:256], in0=otmp[:, 0:256], in1=otmp[:, 0:256], op=ALU.mult),
                     reads=[B_otmp], writes=[B_osq])
                P.op("pe", lambda e: _mm_chain(e, psum[6][:, 0:NS], [(onesb[:, :], osq[:, hp * 64:(hp + 1) * 64]) for hp in range(4)]),
                     reads=[B_osq, B_const], writes=[psb[6]])
                P.op("dve", lambda e: e.tensor_copy(out=ssb[:, QS], in_=psum[6][:, 0:NS]), reads=[psb[6]], writes=[B_ssb])

            sblocks = [("new", None)] + [("c", b) for b in range(15, -1, -1)]
            for si, (kind, b) in enumerate(sblocks):
                ks = 128
                first, last = si == 0, si == len(sblocks) - 1

                def ksrc(h, kind=kind, b=b):
                    hp, hh = h // 2, h % 2
                    if kind == "new":
                        return KTs[:, hp, :]
                    return KTc[:, hp, b * 128:(b + 1) * 128]

                def vsrc(h, kind=kind, b=b):
                    if kind == "new":
                        return Vs[:, h * 64:(h + 1) * 64]
                    return Vc[:, b, h * 64:(h + 1) * 64]

                def zfn(e, ps, ks=ks, kind=kind, ksrc=ksrc):
                    ins = None
                    for h in range(8):
                        hp, hh = h // 2, h % 2
                        o = ps[0:ks, h * 64:(h + 1) * 64]
                        ins = e.matmul(o, lhsT=ksrc(h), rhs=QTs8[:, h, :], start=True, stop=(kind != "new"))
                        if kind == "new":
                            ins = e.matmul(o, lhsT=identb[:, :], rhs=ngmb[:, 0:NS], start=False, stop=True)
                    return ins

                def Efn(e, ps, Lpt, Lsb, ks=ks, kind=kind, ksrc=ksrc):
                    ins = None
                    for h in range(8):
                        hp, hh = h // 2, h % 2
                        hc = slice(h * 64, (h + 1) * 64)
                        o = ps[0:ks, hc]
                        prs = [(ksrc(h), QTs8[:, h, :]), (tmb[0:ks, 0:ks], Lpt[0:ks, hc])]
                        if Lsb is not None:
                            prs.append((negonesb[:, 0:ks], Lsb[:, hc]))
                        if kind == "new":
                            prs.append((identb[:, :], ngmb[:, 0:NS]))
                        ins = _mm_chain(e, o, prs)
                    return ins

                def avfn(e, Att, first, last, ks=ks, vsrc=vsrc):
                    ins = None
                    for h in range(8):
                        hp, hh = h // 2, h % 2
                        ins = e.matmul(psum[4][hh * 64:(hh + 1) * 64, hp * 128:hp * 128 + 64], lhsT=vsrc(h),
                                       rhs=Att[0:ks, h * 64:(h + 1) * 64], start=(first and hp == 0), stop=(last and hp == 3),
                                       tile_position=(0, hh * 64))
                    return ins
                attn_step(zfn, Efn, avfn, ks, 0, first, last, [B_KTc, B_Vc, B_QTs8], psb[4], post=(sample_epilogue if last else None))
            attn_flush()
            ctr["o"] += 1
            P.barrier()

            if MIX_STOP <= 3:
                return
            rcvVv = rcvV.rearrange("(b p) c -> p b c", p=128)
            sndVv = sndV.rearrange("(b p) c -> p b c", p=128)
            for hp in range(4):
                s = hp % 2
                P.dma(lambda e, s=s, hp=hp: e.dma_start(out=KThp[s][:, 0:NP], in_=rcvK[:, hp, :]), "kt%d" % s, reads=[B_rcv], writes=[B_KThp[s]])
                P.dma(lambda e, s=s, hp=hp: e.dma_start(out=KThp[s][:, NP:2 * NP], in_=sndK[:, hp, :]), "kt%d" % s, reads=[B_snd], writes=[B_KThp[s]])
                P.dma(lambda e, s=s, hp=hp: e.dma_start(out=Vhp[s][:, 0:16, :], in_=rcvVv[:, :, hp * 128:(hp + 1) * 128]), "vt%d" % s,
                      reads=[B_rcv], writes=[B_Vhp[s]])
                P.dma(lambda e, s=s, hp=hp: e.dma_start(out=Vhp[s][:, 16:32, :], in_=sndVv[:, :, hp * 128:(hp + 1) * 128]), "vt%d" % s,
                      reads=[B_snd], writes=[B_Vhp[s]])
                P.op("pool", lambda e, s=s: e.tensor_scalar(out=Vhp[s][:, 0:16, :], in0=Vhp[s][:, 0:16, :], scalar1=cols[:, C_FLAG:C_FLAG + 1],
                                                            scalar2=None, op0=ALU.mult), reads=[B_const], writes=[B_Vhp[s]])
                for j in range(4):
                    q0 = j * 512
                    ob = 4 + ctr["o"] % 2
                    def epilogue(hp=hp, q0=q0, ob=ob):
                        P.op("dve", lambda e: e.tensor_scalar(
                            out=mT[:, hp, q0:q0 + 512], in0=psum[ob][:, :], scalar1=cols[:, C_GSB + hp:C_GSB + hp + 1], scalar2=None, op0=ALU.mult),
                            reads=[psb[ob], B_const], writes=[B_mT])
                        P.op("dve", lambda e: e.tensor_copy(out=otmp[:, :], in_=psum[ob][:, :]), reads=[psb[ob]], writes=[B_otmp])
                        P.op("pool", lambda e: e.tensor_tensor(out=osq[:, :], in0=otmp[:, :], in1=otmp[:, :], op=ALU.mult),
                             reads=[B_otmp], writes=[B_osq])
                        P.op("pe", lambda e: e.matmul(psum[6][:, :], lhsT=onesb[:, :], rhs=osq[:, :], start=True, stop=True),
                             reads=[B_osq, B_const], writes=[psb[6]])
                        if hp == 0:
                            P.op("dve", lambda e: e.tensor_copy(out=ssb[:, q0:q0 + 512], in_=psum[6][:, :]), reads=[psb[6]], writes=[B_ssb])
                        else:
                            P.op("dve", lambda e: e.tensor_tensor(out=ssb[:, q0:q0 + 512], in0=ssb[:, q0:q0 + 512], in1=psum[6][:, :], op=ALU.add),
                                 reads=[psb[6]], writes=[B_ssb])

                    for hh in range(2):
                        rows_ = slice(hh * 64, (hh + 1) * 64)
                        qi = ctr["q"] % 2
                        ctr["q"] += 1
                        qm = QTm[qi]

                        def build_q(qm=qm, hh=hh, rows_=rows_, hp=hp, q0=q0, qi=qi):
                            P.op("pool", lambda e: e.memset(qm[(1 - hh) * 64:(2 - hh) * 64, :], 0.0), writes=[B_QTm[qi]])
                            P.op("pool", lambda e: e.tensor_copy(out=qm[rows_, :], in_=QT[rows_, hp, q0:q0 + 512]),
                                 reads=[B_QT], writes=[B_QTm[qi]])
                        blocks = [(16 + 4 * j + m, m) for m in (3, 2, 1, 0)] + [(bb, None) for bb in range(16 + 4 * j - 1, -1, -1)]
                        for bi, (bb, m) in enumerate(blocks):
                            first, last = bi == 0, bi == len(blocks) - 1
                            a0 = 0 if m is None else 128 * m
                            kl = KThp[s][:, bb * 128:(bb + 1) * 128]
                            if m is None:
                                rngs = [(0, 512, False)]
                            else:
                                rngs = [(a0, a0 + 128, True)] + ([(a0 + 128, 512, False)] if a0 + 128 < 512 else [])

                            def zfn(e, ps, kl=kl, rngs=rngs, qm=qm):
                                ins = None
                                for (x0, x1, msk) in rngs:
                                    prs = [(kl, qm[:, x0:x1])]
                                    if msk:
                                        prs.append((identb[:, :], ngmb[:, :]))
                                    ins = _mm_chain(e, ps[:, x0:x1], prs)
                                return ins

                            def Efn(e, ps, Lpt, Lsb, kl=kl, rngs=rngs, qm=qm):
                                ins = None
                                for (x0, x1, msk) in rngs:
                                    prs = [(kl, qm[:, x0:x1]), (tmb[:, :], Lpt[:, x0:x1])]
                                    if Lsb is not None:
                                        prs.append((negonesb[:, :], Lsb[:, x0:x1]))
                                    if msk:
                                        prs.append((identb[:, :], ngmb[:, :]))
                                    ins = _mm_chain(e, ps[:, x0:x1], prs)
                                return ins

                            def avfn(e, Att, first, last, s=s, bb=bb, rows_=rows_, hh=hh, ob=ob):
                                return e.matmul(psum[ob][rows_, 0:512], lhsT=Vhp[s][:, bb, hh * 64:(hh + 1) * 64], rhs=Att[:, 0:512],
                                                start=first, stop=last, tile_position=(0, hh * 64))
                            attn_step(zfn, Efn, avfn, 128, a0, first, last, [B_KThp[s], B_Vhp[s], B_QTm[qi]], psb[ob],
                                      pre=(build_q if first else None), post=(epilogue if (last and hh == 1) else None))
                    ctr["o"] += 1
            attn_flush()
            P.barrier()

            if MIX_STOP <= 4:
                return
            P.op("act", lambda e: e.activation(out=ssb[:, :], in_=ssb[:, :], func=AF.Ln, bias=epscol[:], scale=1.0 / 512),
                 reads=[B_const], writes=[B_ssb])
            P.op("act", lambda e: e.activation(out=ssb[:, :], in_=ssb[:, :], func=AF.Exp, scale=-0.5), writes=[B_ssb])
            for gstart, sgs in GROUPS:
                A.reset(keep)
                ysb = A.f32(128, DC, GW)
                wo32 = [A.f32(128, DC, 128) for _ in range(2)]
                wob = [A.bf16(128, DC, 128) for _ in range(2)]
                sqo = A.bf16(128, DC, 512)
                rso = A.f32(128, 512)
                tmpo = [A.f32(128, 512) for _ in range(2)]
                B_ysb = [Buf() for _ in sgs]
                B_wo32, B_wob, B_sqo, B_rso, B_tmpo = [Buf(), Buf()], [Buf(), Buf()], Buf(), Buf(), [Buf(), Buf()]
                k = 0
                for dmc in range(DC):
                    s = dmc % 2
                    P.dma(lambda e, s=s, dmc=dmc: e.dma_start(out=wo32[s], in_=w_outv[:, :, dmc * 128:(dmc + 1) * 128]), "wo%d" % s, writes=[B_wo32[s]])
                    P.op("act", lambda e, s=s: e.activation(out=wob[s], in_=wo32[s], func=AF.Copy), reads=[B_wo32[s]], writes=[B_wob[s]])
                    for si, (c0, n) in enumerate(sgs):
                        lc = c0 - gstart
                        ba, bb_ = (2 * k) % 6, (2 * k + 1) % 6
                        t2 = k % 2
                        k += 1
                        P.op("pe", lambda e, s=s, c0=c0, n=n, ba=ba: _mm_chain(
                            e, psum[ba][:, 0:n], [(wob[s][:, fc, :], mT[:, fc, c0:c0 + n]) for fc in range(4)]),
                            reads=[B_wob[s], B_mT], writes=[psb[ba]])
                        P.op("pe", lambda e, s=s, c0=c0, n=n, bb_=bb_: _mm_chain(
                            e, psum[bb_][:, 0:n], [(wob[s][:, fc, :], mT[:, fc, c0:c0 + n]) for fc in range(4, 8)]),
                            reads=[B_wob[s], B_mT], writes=[psb[bb_]])
                        P.op("dve", lambda e, c0=c0, n=n, ba=ba, t2=t2: e.tensor_tensor(
                            out=tmpo[t2][:, 0:n], in0=psum[ba][:, 0:n], in1=ssb[:, c0:c0 + n], op=ALU.mult),
                            reads=[psb[ba], B_ssb], writes=[B_tmpo[t2]])
                        P.op("dve", lambda e, n=n, bb_=bb_, t2=t2, dmc=dmc, lc=lc: e.tensor_tensor(
                            out=ysb[:, dmc, lc:lc + n], in0=tmpo[t2][:, 0:n], in1=psum[bb_][:, 0:n], op=ALU.add),
                            reads=[B_tmpo[t2], psb[bb_]], writes=[B_ysb[si]])
                postnorm_residual(gstart, sgs, C_MPOST, ysb, B_ysb, [sqo, sqo], [B_sqo, B_sqo], [rso, rso], [B_rso, B_rso])
                P.barrier()

        if not SKIP_FFN1:
            ffn("ffn1", C_F1PRE, C_F1POST)

        if STOP_AFTER != "ffn1":
            mix_phase()
            if STOP_AFTER != "mix":
                ffn("ffn2", C_F2PRE, C_F2POST)

        A.reset()
        yst = [A.f32(128, D) for _ in range(4)]
        B_yst = [Buf() for _ in range(4)]
        for ti, (r0, rows) in enumerate(tiles):
            s = ti % 4
            sgi = min(r0 // 512, 4)
            for hb in range(2):
                bank = (2 * ti + hb) % 4

                def tr(e, hb=hb, bank=bank, rows=rows, r0=r0):
                    ins = None
                    for q in range(4):
                        dc = hb * 4 + q
                        ins = e.transpose(out=psum[bank][0:rows, q * 128:(q + 1) * 128],
                                          in_=xT[:, dc, r0:r0 + rows], identity=identf[:, :])
                    return ins
                P.op("pe", tr, reads=[B_xT[sgi], B_const], writes=[psb[bank]])
                eng = "dve" if hb == 0 else "act"

                def ev(e, s=s, hb=hb, bank=bank, rows=rows, eng=eng):
                    dst = yst[s][0:rows, hb * 512:(hb + 1) * 512]
                    if eng == "dve":
                        return e.tensor_copy(out=dst, in_=psum[bank][0:rows, :])
                    return e.activation(out=dst, in_=psum[bank][0:rows, :], func=AF.Copy)
                P.op(eng, ev, reads=[psb[bank]], writes=[B_yst[s]])
            P.dma(lambda e, s=s, r0=r0, rows=rows: e.dma_start(out=y_out[r0:r0 + rows, :], in_=yst[s][0:rows, :]),
                  "oy%d" % s, reads=[B_yst[s]])

        names = sorted(P.cnt.keys())
        sems = {k: es.enter_context(nc.semaphore(k)) for k in names}
        block = es.enter_context(nc.Block())
        P.emit(nc, block, sems)
    return nc


_NC_CACHE = {}


def _consts():
    ident = np.eye(128, dtype=np.float32)
    j = np.arange(128)
    tm = -(j[:, None] >= j[None, :]).astype(np.float32)
    ngm = np.where(j[:, None] >= j[None, :], NEG, 0.0).astype(np.float32)
    mskT = ((j[:, None] // 64) <= (j[None, :] // 64)).astype(np.float32)
    return ident, tm, ngm, mskT


def _col(v, n):
    return np.ascontiguousarray(np.asarray(v, np.float32).reshape(n, 128).T)


def kernel(x_prompt, x_sample, cache_k_sb, cache_v_sb,
           ffn1_pre_g, ffn1_post_g, ffn1_w_gate, ffn1_w_up, ffn1_w_down,
           mix_pre_g, mix_post_g, w_in, sgu_ln_g, sgu_ln_b, sgu_w_s, sgu_b_s,
           g_out_sb, g_out_sgu, w_out,
           ffn2_pre_g, ffn2_post_g, ffn2_w_gate, ffn2_w_up, ffn2_w_down):
    f = lambda a: np.ascontiguousarray(np.asarray(a, dtype=np.float32))
    x_prompt, x_sample = f(x_prompt), f(x_sample)
    ck, cv = f(cache_k_sb), f(cache_v_sb)
    ident, tm, ngm, mskT = _consts()
    shared = {
        "ffn1_wg": f(ffn1_w_gate)[0], "ffn1_wu": f(ffn1_w_up)[0], "ffn1_wd": f(ffn1_w_down)[0],
        "ffn2_wg": f(ffn2_w_gate)[0], "ffn2_wu": f(ffn2_w_up)[0], "ffn2_wd": f(ffn2_w_down)[0],
        "w_in": f(w_in)[0], "w_out": f(w_out)[0],
        "lng_b": np.ascontiguousarray(np.broadcast_to(f(sgu_ln_g)[0][None, :], (128, 512))),
        "lnb_b": np.ascontiguousarray(np.broadcast_to(f(sgu_ln_b)[0][None, :], (128, 512))),
        "wsT": np.ascontiguousarray(f(sgu_w_s)[0].transpose(2, 0, 1)),
        "mskT": mskT,
        "bsT": np.ascontiguousarray(np.repeat(f(sgu_b_s)[0].reshape(4, 2, 1, 128), 64, axis=2)
                                    .reshape(4, 128, 128).transpose(1, 0, 2)),
        "ident": ident, "tm": tm, "ngm": ngm,
    }
    in_maps = []
    for c in range(8):
        b, half = c // 2, c % 2
        colsv = np.zeros((128, NCOL), np.float32)
        colsv[:, 0:8] = _col(ffn1_pre_g, 8)
        colsv[:, 8:16] = _col(ffn1_post_g, 8)
        colsv[:, 16:24] = _col(mix_pre_g, 8)
        colsv[:, 24:32] = _col(mix_post_g, 8)
        colsv[:, 32:40] = _col(ffn2_pre_g, 8)
        colsv[:, 40:48] = _col(ffn2_post_g, 8)
        colsv[:, 48:52] = _col(g_out_sb, 4)
        colsv[:, 52:56] = _col(g_out_sgu, 4)
        colsv[:, 56] = float(half)
        m = dict(shared)
        m["x_in"] = np.concatenate([x_prompt[b, half * NP:(half + 1) * NP], x_sample[c]], axis=0)
        m["ck"] = ck[0, c].reshape(PAST, 512)
        m["cv"] = cv[0, c].reshape(PAST, 512)
        m["cols"] = colsv
        in_maps.append(m)
    if "nc" not in _NC_CACHE:
        _NC_CACHE["nc"] = build_program()
    res = run_bass_kernel_spmd(_NC_CACHE["nc"], in_maps, core_ids=list(range(8)))
    R = res.results
    y_p = np.zeros((4, 4096, D), np.float32)
    y_s = np.zeros((8, NS, D), np.float32)
    k_p = np.zeros((1, 4, 4096, 8, 64), np.float32)
    v_p = np.zeros((1, 4, 4096, 8, 64), np.float32)
    k_s = np.zeros((1, 8, NS, 8, 64), np.float32)
    v_s = np.zeros((1, 8, NS, 8, 64), np.float32)
    g_s = np.zeros((1, 8, NS, 8, 64), np.float32)
    for c in range(8):
        b, half = c // 2, c % 2
        y = np.asarray(R[c]["y"])
        ko = np.asarray(R[c]["k_o"])
        vo = np.asarray(R[c]["v_o"])
        go = np.asarray(R[c]["g_o"])
        y_p[b, half * NP:(half + 1) * NP] = y[0:NP]
        y_s[c] = y[NP:NT]
        k_p[0, b, half * NP:(half + 1) * NP] = ko[0:NP].reshape(NP, 8, 64)
        v_p[0, b, half * NP:(half + 1) * NP] = vo[0:NP].reshape(NP, 8, 64)
        k_s[0, c] = ko[NP:NT].reshape(NS, 8, 64)
        v_s[0, c] = vo[NP:NT].reshape(NS, 8, 64)
        g_s[0, c] = go.reshape(NS, 8, 64)
    return (y_p, y_s, k_p, v_p, k_s, v_s, g_s)
```
